# Optimizing a Trainium2 kernel written in Bass

```python
import jax, jax.numpy as jnp
from jax import lax
import numpy as np

D_MODEL = 1024
BATCH = 8
SEQ = 8192
DEPTH = 1

CONV_DIM = 512
CONV_GROUPS = 8
CONV_K = 3
RWKV_HEADS = 8
RWKV_HEAD_DIM = 64
RWKV_DIM = RWKV_HEADS * RWKV_HEAD_DIM
DECAY_RANK = 64
ICLR_RANK = 64
GATE_RANK = 128
DECAY_SCALE = 0.6065306597126334
GN_EPS = 64e-5
CONV_PROJ = 3 * CONV_DIM
RWKV_PROJ = 3 * RWKV_DIM + DECAY_RANK + ICLR_RANK + GATE_RANK
GATE_PROJ = 2 * D_MODEL
IN_PROJ = CONV_PROJ + RWKV_PROJ + GATE_PROJ
PEER_HEADS = 8
PEER_NKEYS = 128
PEER_EXPERTS = PEER_NKEYS * PEER_NKEYS
PEER_QDIM = 256
PEER_HALF = PEER_QDIM // 2
PEER_TOPK = 16
PEER_TOKEN_BLOCK = 128
PLE_DIM = 256
NORM_EPS = 1e-6

kernel_name = "hybrid_conv_rwkv7_peer_block"


def rmsnorm(x, g):
    xf = x.astype(jnp.float32)
    y = xf * lax.rsqrt(jnp.mean(xf * xf, axis=-1, keepdims=True) + NORM_EPS)
    return y.astype(x.dtype) * g


def short_conv_mixer(z, conv_w, conv_b):
    gate_b, gate_c, x_in = jnp.split(z, 3, axis=-1)
    u = gate_c * x_in
    y = lax.conv_general_dilated(u, conv_w[:, None, :], window_strides=(1,), padding=[(CONV_K - 1, 0)],
                                 dimension_numbers=('NWC', 'WIO', 'NWC'), feature_group_count=CONV_DIM)
    return gate_b * (y + conv_b)


def rwkv7_scan(r, w, k, v, kk, a):
    bsz, _, nh, nd = r.shape
    xs = tuple(jnp.moveaxis(t, 1, 0) for t in (r, w, k, v, kk, a))

    def step(S, inp):
        r_t, w_t, k_t, v_t, kk_t, a_t = inp
        s_kk = jnp.einsum('bhvk,bhk->bhv', S, kk_t)
        S = (S * w_t[:, :, None, :] - s_kk[..., None] * (kk_t * a_t)[:, :, None, :]
             + v_t[..., None] * k_t[:, :, None, :])
        return S, jnp.einsum('bhvk,bhk->bhv', S, r_t)

    s0 = jnp.zeros((bsz, nh, nd, nd), jnp.float32)
    _, ys = lax.scan(step, s0, xs)
    return jnp.moveaxis(ys, 0, 1)


def rwkv7_mixer(z, shift_mu, w0, w_up, a0, a_up, g_up, k_k, k_a, r_k, ln_g, ln_b):
    bsz, seq, _ = z.shape
    f32 = jnp.float32
    z_prev = jnp.pad(z, ((0, 0), (1, 0), (0, 0)))[:, :-1]
    z = z + shift_mu * (z_prev - z)
    r, k, v, wd, ad, gd = jnp.split(
        z, [RWKV_DIM, 2 * RWKV_DIM, 3 * RWKV_DIM, 3 * RWKV_DIM + DECAY_RANK,
            3 * RWKV_DIM + DECAY_RANK + ICLR_RANK], axis=-1)
    d = (w0 + jnp.tanh(wd) @ w_up).astype(f32)
    decay = jnp.exp(-DECAY_SCALE * jax.nn.sigmoid(d))
    a = jax.nn.sigmoid(a0 + ad @ a_up)
    g = jax.nn.sigmoid(gd) @ g_up
    heads = lambda t: t.reshape(bsz, seq, RWKV_HEADS, RWKV_HEAD_DIM).astype(f32)
    kk = heads(k * k_k)
    kk = kk * lax.rsqrt(jnp.sum(kk * kk, axis=-1, keepdims=True) + 1e-12)
    k = k * (1.0 + (a - 1.0) * k_a)
    rh, kh, vh, ah, wh = heads(r), heads(k), heads(v), heads(a), heads(decay)
    y = rwkv7_scan(rh, wh, kh, vh, kk, ah)
    mu = jnp.mean(y, axis=-1, keepdims=True)
    var = jnp.mean(jnp.square(y - mu), axis=-1, keepdims=True)
    y = ((y - mu) * lax.rsqrt(var + GN_EPS)).reshape(bsz, seq, RWKV_DIM) * ln_g + ln_b
    bonus = jnp.sum(rh * kh * r_k.astype(f32), axis=-1, keepdims=True) * vh
    y = (y + bonus.reshape(bsz, seq, RWKV_DIM)) * g
    return y.astype(z.dtype)


def peer_ffn(u, wq, subkeys, tab_u, tab_v):
    bsz, seq, dm = u.shape
    blocks = u.reshape(-1, PEER_TOKEN_BLOCK, dm)

    def one_block(xb):
        tb = xb.shape[0]
        q = (xb @ wq).reshape(tb, PEER_HEADS, 2, PEER_HALF)
        s = jnp.einsum('thcd,hcnd->thcn', q, subkeys)
        sv, si = lax.top_k(s, PEER_TOPK)
        cand_s = (sv[:, :, 0, :, None] + sv[:, :, 1, None, :]).reshape(tb, PEER_HEADS, PEER_TOPK * PEER_TOPK)
        cand_i = (si[:, :, 0, :, None] * PEER_NKEYS + si[:, :, 1, None, :]).reshape(tb, PEER_HEADS, PEER_TOPK * PEER_TOPK)
        top_s, pos = lax.top_k(cand_s, PEER_TOPK)
        idx = jnp.take_along_axis(cand_i, pos, axis=-1)
        gate = jax.nn.softmax(top_s.astype(jnp.float32), axis=-1).astype(xb.dtype)
        hid = jax.nn.gelu(jnp.einsum('td,thkd->thk', xb, tab_u[idx]))
        return jnp.einsum('thk,thkd->td', gate * hid, tab_v[idx])

    return lax.map(one_block, blocks).reshape(bsz, seq, dm)


def setup_inputs(seed: int = 0) -> dict:
    key = jax.random.key(seed)
    ks = iter(jax.random.split(key, 40))
    nrm = lambda shape, scale: jax.random.normal(next(ks), shape, jnp.float32) * scale
    L = DEPTH
    return {
        'x': nrm((BATCH, SEQ, D_MODEL), 1.0),
        'p': nrm((DEPTH, BATCH, SEQ, PLE_DIM), 1.0),
        'norm_mix_g': 1.0 + nrm((L, D_MODEL), 0.02),
        'w_in': nrm((L, D_MODEL, IN_PROJ), D_MODEL ** -0.5),
        'conv_w': nrm((L, CONV_K, CONV_DIM), CONV_K ** -0.5),
        'conv_b': nrm((L, CONV_DIM), 0.01),
        'shift_mu': jax.random.uniform(next(ks), (L, RWKV_PROJ), jnp.float32),
        'w0': nrm((L, RWKV_DIM), 0.5),
        'w_up': nrm((L, DECAY_RANK, RWKV_DIM), 0.5 * DECAY_RANK ** -0.5),
        'a0': nrm((L, RWKV_DIM), 0.5),
        'a_up': nrm((L, ICLR_RANK, RWKV_DIM), 0.5 * ICLR_RANK ** -0.5),
        'g_up': nrm((L, GATE_RANK, RWKV_DIM), GATE_RANK ** -0.5),
        'k_k': 0.85 + nrm((L, RWKV_DIM), 0.05),
        'k_a': 1.0 + nrm((L, RWKV_DIM), 0.05),
        'r_k': nrm((L, RWKV_HEADS, RWKV_HEAD_DIM), 0.1),
        'ln_x_g': 1.0 + nrm((L, RWKV_DIM), 0.02),
        'ln_x_b': nrm((L, RWKV_DIM), 0.01),
        'w_branch_a': nrm((L, CONV_DIM, D_MODEL), CONV_DIM ** -0.5),
        'w_branch_b': nrm((L, RWKV_DIM, D_MODEL), RWKV_DIM ** -0.5),
        'w_out': nrm((L, D_MODEL, D_MODEL), D_MODEL ** -0.5),
        'norm_ffn_g': 1.0 + nrm((L, D_MODEL), 0.02),
        'peer_wq': nrm((L, D_MODEL, PEER_HEADS * PEER_QDIM), D_MODEL ** -0.5),
        'peer_subkeys': nrm((L, PEER_HEADS, 2, PEER_NKEYS, PEER_HALF), PEER_HALF ** -0.5),
        'peer_u': nrm((L, PEER_EXPERTS, D_MODEL), D_MODEL ** -0.5),
        'peer_v': nrm((L, PEER_EXPERTS, D_MODEL), (PEER_HEADS * PEER_TOPK) ** -0.5),
        'norm_ple_g': 1.0 + nrm((L, D_MODEL), 0.02),
        'ple_gate_w': nrm((L, D_MODEL, D_MODEL), D_MODEL ** -0.5),
        'ple_proj_w': nrm((L, PLE_DIM, D_MODEL), PLE_DIM ** -0.5),
        'final_norm_g': 1.0 + nrm((D_MODEL,), 0.02),
    }


def reference(x, p, norm_mix_g, w_in, conv_w, conv_b, shift_mu, w0, w_up, a0, a_up, g_up,
              k_k, k_a, r_k, ln_x_g, ln_x_b, w_branch_a, w_branch_b, w_out, norm_ffn_g,
              peer_wq, peer_subkeys, peer_u, peer_v, norm_ple_g, ple_gate_w, ple_proj_w,
              final_norm_g):
    h = x
    for i in range(DEPTH):
        xn = rmsnorm(h, norm_mix_g[i])
        z = xn @ w_in[i]
        z_conv = z[..., :CONV_PROJ]
        z_rwkv = z[..., CONV_PROJ:CONV_PROJ + RWKV_PROJ]
        gate_a, gate_b = jnp.split(z[..., CONV_PROJ + RWKV_PROJ:], 2, axis=-1)
        y_a = short_conv_mixer(z_conv, conv_w[i], conv_b[i]) @ w_branch_a[i]
        y_b = rwkv7_mixer(z_rwkv, shift_mu[i], w0[i], w_up[i], a0[i], a_up[i], g_up[i],
                          k_k[i], k_a[i], r_k[i], ln_x_g[i], ln_x_b[i]) @ w_branch_b[i]
        merged = jax.nn.sigmoid(gate_a) * y_a + jax.nn.sigmoid(gate_b) * y_b
        h = h + merged @ w_out[i]
        h = h + peer_ffn(rmsnorm(h, norm_ffn_g[i]), peer_wq[i], peer_subkeys[i], peer_u[i], peer_v[i])
        ple_gate = jax.nn.sigmoid(rmsnorm(h, norm_ple_g[i]) @ ple_gate_w[i])
        h = h + ple_gate * (p[i] @ ple_proj_w[i])
    return rmsnorm(h, final_norm_g)
```

```python
import numpy as np
from contextlib import ExitStack
import concourse.bass as bass
import concourse.mybir as mybir
from concourse.bass_utils import run_bass_kernel_spmd

F32 = mybir.dt.float32
BF16 = mybir.dt.bfloat16
U32 = mybir.dt.uint32
I32 = mybir.dt.int32
ALU = mybir.AluOpType
AF = mybir.ActivationFunctionType
AX = mybir.AxisListType

D = 1024
NEXP = 16384
SEM_EPOCH = 30000
ATTACH_WAIT = True


class Reg:
    __slots__ = ("w", "r")

    def __init__(self):
        self.w = None
        self.r = {}


class Tile:
    def __init__(self, t):
        self.t = t
        self.reg = Reg()

    def sub(self):
        return Tile(self.t)

    def __getitem__(self, k):
        return self.t[k]


class Ctx:
    def __init__(self, nc):
        self.nc = nc
        self.E = {"pe": nc.tensor, "act": nc.scalar, "dve": nc.vector, "pool": nc.gpsimd, "sp": nc.sync}
        self.nsem = 0
        self.sem = {}
        self.semkey = {}
        self.cnt = {}
        self.known = {e: {} for e in self.E}
        self.pend_r = {e: [] for e in self.E}
        self.pend_w = {e: [] for e in self.E}
        for e in self.E:
            self._newsem(e)
        self.dq = {}
        self.alldma = {}
        self.psb = []
        self.psi = 0
        self.defer = None

    def _alloc(self, name):
        h = self.nc.alloc_semaphore(name=f"{name}_{self.nsem}")
        self.nsem += 1
        return (f"{name}_{self.nsem}", h)

    def _newsem(self, e):
        key, h = self._alloc("s" + e)
        self.sem[e] = h
        self.semkey[e] = key
        self.cnt[e] = 0

    def _wait(self, e, ev):
        key, h, v = ev
        if self.known[e].get(key, 0) >= v:
            return
        self.known[e][key] = v
        if self.defer is not None:
            self.defer.append((h, v))
        else:
            self.E[e].wait_ge(h, v)

    def _flush_waits(self, e, keep_last):
        d = self.defer
        self.defer = None
        last = None
        if keep_last and d:
            last = d.pop()
        for (h, v) in d:
            self.E[e].wait_ge(h, v)
        return last

    def _need(self, e, ev, kind, is_dma):
        if (not is_dma) and ev[0] == self.semkey[e]:
            if e == "pe":
                return
        self._wait(e, ev)

    def _deps(self, e, reads, writes, is_dma=False):
        for r in reads:
            if r.reg.w is not None:
                self._need(e, r.reg.w, "raw", is_dma)
        for w in writes:
            if w.reg.w is not None:
                self._need(e, w.reg.w, "waw", is_dma)
            for ev in list(w.reg.r.values()):
                self._need(e, ev, "war", is_dma)

    def _commit(self, ev, reads, writes):
        for w in writes:
            w.reg.w = ev
            w.reg.r = {}
        for r in reads:
            r.reg.r[ev[0]] = ev

    def op(self, e, fn, r=(), w=(), inc=True):
        self.defer = []
        self._deps(e, r, w)
        last = self._flush_waits(e, ATTACH_WAIT)
        ins = fn(self.E[e])
        if last is not None:
            ins._wait_ge(last[0], last[1])
        if not inc:
            self.pend_r[e].extend(r)
            self.pend_w[e].extend(w)
            return None
        if self.cnt[e] >= SEM_EPOCH:
            self._newsem(e)
        self.cnt[e] += 1
        ins.then_inc(self.sem[e], 1)
        ev = (self.semkey[e], self.sem[e], self.cnt[e])
        self._commit(ev, list(r) + self.pend_r[e], list(w) + self.pend_w[e])
        self.pend_r[e] = []
        self.pend_w[e] = []
        return ev

    def dma(self, q, out, in_, r=(), w=(), fn=None, nslots=12):
        self._deps(q, r, w, is_dma=True)
        pool = self.dq.setdefault(q, {"slots": [None] * nslots, "i": 0})
        i = pool["i"] % nslots
        pool["i"] += 1
        slot = pool["slots"][i]
        if slot is None:
            key, h = self._alloc("d" + q)
            slot = [key, h, 0]
            pool["slots"][i] = slot
        if slot[2] > 0:
            self._wait(q, (slot[0], slot[1], 16 * slot[2]))
        if 16 * (slot[2] + 1) > SEM_EPOCH:
            key, h = self._alloc("d" + q)
            slot[0], slot[1], slot[2] = key, h, 0
        if fn is None:
            ins = self.E[q].dma_start(out=out, in_=in_)
        else:
            ins = fn(self.E[q])
        ins.then_inc(slot[1], 16)
        slot[2] += 1
        ev = (slot[0], slot[1], 16 * slot[2])
        self.alldma[slot[0]] = ev
        self._commit(ev, r, w)
        return ev

    def barrier(self):
        evs = [(self.semkey[e], self.sem[e], self.cnt[e]) for e in self.E if self.cnt[e] > 0]
        evs += list(self.alldma.values())
        for e in self.E:
            for ev in evs:
                if ev[0] == self.semkey[e] and e == "pe":
                    continue
                self._wait(e, ev)

    def init_psum(self, st):
        for i in range(8):
            self.psb.append(Tile(st.enter_context(self.nc.psum_tensor(f"psb{i}", [128, 512], F32))))

    def ps(self):
        t = self.psb[self.psi % 8]
        self.psi += 1
        return t


def build(T, debug=False, phases="AB12"):
    NT = T // 128
    nc = bass.Bass("TRN2", target_bir_lowering=False)
    K = Ctx(nc)

    def din(name, shape, dt=F32):
        return nc.dram_tensor(name, list(shape), dt, kind="ExternalInput").ap()

    skind = "ExternalOutput" if debug else "Internal"

    def dscr(name, shape, dt=F32):
        return nc.dram_tensor(name, list(shape), dt, kind=skind).ap()

    x = din("x", [T, D]); p_in = din("p", [T, 256])
    w_in = din("w_in", [D, 5376]); conv_w = din("conv_w", [3, 512]); conv_b = din("conv_b", [512])
    shift_mu = din("shift_mu", [1792]); w0 = din("w0", [512]); w_up = din("w_up", [64, 512])
    a0 = din("a0", [512]); a_up = din("a_up", [64, 512]); g_up = din("g_up", [128, 512])
    k_k = din("k_k", [512]); k_a = din("k_a", [512]); r_k = din("r_k", [512])
    ln_g = din("ln_g", [512]); ln_b = din("ln_b", [512])
    w_a = din("w_a", [512, D]); w_b = din("w_b", [512, D]); w_out = din("w_out", [D, D])
    gffn = din("gffn", [D]); wq = din("wq", [D, 2048]); skT = din("skT", [16, 128, 128])
    tab_u = din("tab_u", [NEXP, D]); tab_v = din("tab_v", [NEXP, D])
    gple = din("gple", [D]); pgw = din("pgw", [D, D]); ppw = din("ppw", [256, D])
    gfin = din("gfin", [D]); gmix = din("gmix", [D])
    ident_d = din("ident", [128, 128]); masks_d = din("masks", [128, 2]); iota_d = din("iota16", [256])
    out = nc.dram_tensor("out", [T, D], F32, kind="ExternalOutput").ap()

    s_kr = dscr("s_kr", [512, T + 1, 4]); s_w = dscr("s_w", [512, T])
    s_bk = dscr("s_bk", [4, 6, T, 128]); s_v = dscr("s_v", [T, 512]); s_g = dscr("s_g", [T, 512])
    s_rk = dscr("s_rk", [T, 8]); s_pa = dscr("s_pa", [D, T], BF16); s_sgb = dscr("s_sgb", [D, T], BF16)
    s_y = dscr("s_y", [T, 512]); s_h1 = dscr("s_h1", [T, D])

    with ExitStack() as top:
        K.init_psum(top)
        cst = top
        ident = Tile(cst.enter_context(nc.sbuf_tensor("ident_sb", [128, 128], F32)))
        identb = Tile(cst.enter_context(nc.sbuf_tensor("identb", [128, 128], BF16)))
        masks = Tile(cst.enter_context(nc.sbuf_tensor("masks_sb", [128, 2], F32)))
        K.dma("sp", ident[:], ident_d, w=[ident])
        K.dma("sp", masks[:], masks_d, w=[masks])
        K.op("dve", lambda e: e.tensor_copy(out=identb[:], in_=ident[:]), r=[ident], w=[identb])

        def sbt(st, name, shape, dt):
            return Tile(st.enter_context(nc.sbuf_tensor(name, list(shape), dt)))

        def rms_rstd(src, ss, rstd, junk, eps=1e-6, n=1024.0):
            K.op("act", lambda e: e.activation(out=junk[:], in_=src[:], func=AF.Square, accum_out=ss[:]),
                 r=[src], w=[junk, ss])
            K.op("dve", lambda e: e.tensor_scalar(out=rstd[:], in0=ss[:], scalar1=1.0 / n, scalar2=eps,
                                                  op0=ALU.mult, op1=ALU.add), r=[ss], w=[rstd])
            K.op("act", lambda e: e.activation(out=rstd[:], in_=rstd[:], func=AF.Sqrt), r=[rstd], w=[rstd])
            K.op("dve", lambda e: e.reciprocal(out=rstd[:], in_=rstd[:]), r=[rstd], w=[rstd])

        def load_cast(st, dst, src_ap, nk, ncol, scale_tile=None, q="sp"):
            stg = [sbt(st, f"stg_{id(dst)}_{i}", [128, ncol], F32) for i in range(2)]
            for kc in range(nk):
                s_ = stg[kc % 2]
                K.dma(q, s_[:], src_ap[kc * 128:(kc + 1) * 128, :], w=[s_])
                if scale_tile is None:
                    K.op("act", lambda e: e.activation(out=dst[:, kc, :], in_=s_[:], func=AF.Copy), r=[s_], w=[dst])
                else:
                    K.op("act", lambda e: e.activation(out=dst[:, kc, :], in_=s_[:], func=AF.Copy,
                                                       scale=scale_tile[:, kc:kc + 1]), r=[s_, scale_tile], w=[dst])

        def bcast(st, name, src_ap, n):
            t = sbt(st, name, [128, n], F32)
            K.dma("sp", t[:], src_ap.partition_broadcast(128), w=[t])
            return t

        def colvec(st, name, src_ap, nk):
            t = sbt(st, name, [128, nk], F32)
            K.dma("sp", t[:], src_ap.rearrange("(c p) -> p c", p=128), w=[t],
                  fn=lambda e: e.dma_start(out=t[:], in_=src_ap.rearrange("(c p) -> p c", p=128),
                                           allow_slow_non_contiguous=True))
            return t

        if "A" in phases:
            with ExitStack() as sa:
                Wc = sbt(sa, "Wc", [128, 8, 1536], BF16)
                W1 = sbt(sa, "W1", [128, 8, 1792], BF16)
                W2 = sbt(sa, "W2", [128, 8, 1792], BF16)
                Wg = sbt(sa, "Wg", [128, 8, 2048], BF16)
                wa = sbt(sa, "wa", [128, 4, 1024], BF16)
                waup = sbt(sa, "waup", [128, 512], BF16)
                gup = sbt(sa, "gup", [128, 512], BF16)
                gm = colvec(sa, "gm", gmix, 8)
                cw = sbt(sa, "cw", [128, 3, 4], F32)
                K.dma("sp", None, None, w=[cw], fn=lambda e: e.dma_start(
                    out=cw[:], in_=conv_w.rearrange("j (c p) -> p j c", p=128), allow_slow_non_contiguous=True))
                cb = colvec(sa, "cb", conv_b, 4)
                w0_b = bcast(sa, "w0_b", w0, 512); a0_b = bcast(sa, "a0_b", a0, 512)
                kk_b = bcast(sa, "kk_b", k_k, 512); ka_b = bcast(sa, "ka_b", k_a, 512)
                rk_b = bcast(sa, "rk_b", r_k, 512)
                with ExitStack() as sp_:
                    mu_b = bcast(sp_, "mu_b", shift_mu, 1792)
                    stg = [sbt(sp_, f"stgw{i}", [128, 5376], F32) for i in range(2)]
                    t2 = sbt(sp_, "t2w", [128, 1792], F32)
                    for kc in range(8):
                        s_ = stg[kc % 2]
                        K.dma("sp", s_[:], w_in[kc * 128:(kc + 1) * 128, :], w=[s_])
                        K.op("act", lambda e: e.activation(out=Wc[:, kc, :], in_=s_[:, 0:1536], func=AF.Copy,
                                                           scale=gm[:, kc:kc + 1]), r=[s_, gm], w=[Wc])
                        K.op("act", lambda e: e.activation(out=Wg[:, kc, :], in_=s_[:, 3328:5376], func=AF.Copy,
                                                           scale=gm[:, kc:kc + 1]), r=[s_, gm], w=[Wg])
                        K.op("dve", lambda e: e.scalar_tensor_tensor(out=t2[:], in0=s_[:, 1536:3328], scalar=gm[:, kc:kc + 1],
                                                                     in1=mu_b[:], op0=ALU.mult, op1=ALU.mult),
                             r=[s_, gm, mu_b], w=[t2])
                        K.op("pool", lambda e: e.tensor_copy(out=W2[:, kc, :], in_=t2[:]), r=[t2], w=[W2])
                        K.op("dve", lambda e: e.scalar_tensor_tensor(out=W1[:, kc, :], in0=s_[:, 1536:3328], scalar=gm[:, kc:kc + 1],
                                                                     in1=t2[:], op0=ALU.mult, op1=ALU.subtract),
                             r=[s_, gm, t2], w=[W1])
                    load_cast(sp_, wa, w_a, 4, 1024)
                    stq = sbt(sp_, "stq", [128, 512], F32)
                    K.dma("sp", stq[0:64, :], w_up, w=[stq])
                    K.dma("sp", stq[64:128, :], a_up, w=[stq])
                    K.op("act", lambda e: e.activation(out=waup[:], in_=stq[:], func=AF.Copy), r=[stq], w=[waup])
                    stq2 = sbt(sp_, "stq2", [128, 512], F32)
                    K.dma("sp", stq2[:], g_up, w=[stq2])
                    K.op("act", lambda e: e.activation(out=gup[:], in_=stq2[:], func=AF.Copy), r=[stq2], w=[gup])
                K.barrier()

                xt = [sbt(sa, f"xt{i}", [128, 1024], F32) for i in range(2)]
                junk = sbt(sa, "junkA", [128, 1024], BF16)
                ss = sbt(sa, "ssA", [128, 1], F32); rstd = sbt(sa, "rstdA", [128, 1], F32)
                xn = sbt(sa, "xnA", [128, 1024], BF16)
                xnT = [sbt(sa, f"xnT{i}", [128, 8, 129], BF16) for i in range(2)]
                gbs = sbt(sa, "gbs", [128, 4, 128], F32); gcs = sbt(sa, "gcs", [128, 4, 128], F32)
                u = sbt(sa, "uconv", [128, 4, 130], F32)
                yt = sbt(sa, "yconv_t", [128, 4, 128], F32)
                yc = sbt(sa, "yconv", [128, 4, 128], BF16)
                sga = sbt(sa, "sga", [128, 4, 128], F32)
                pa_st = sbt(sa, "pa_st", [128, 8, 128], BF16); sgb_st = sbt(sa, "sgb_st", [128, 8, 128], BF16)
                waT = sbt(sa, "waT", [128, 128], BF16); sgd = sbt(sa, "sgd", [128, 128], BF16)
                t_w = sbt(sa, "t_w", [128, 512], F32); t_a = sbt(sa, "t_a", [128, 512], F32)
                t_kk = sbt(sa, "t_kk", [128, 512], F32); t_sq = sbt(sa, "t_sq", [128, 512], F32)
                t_r = sbt(sa, "t_r", [128, 512], F32); t_v = sbt(sa, "t_v", [128, 512], F32)
                t_g = sbt(sa, "t_g", [128, 512], F32); t_km = sbt(sa, "t_km", [128, 512], F32)
                t_3 = sbt(sa, "t_3", [128, 512], F32); t_4 = sbt(sa, "t_4", [128, 512], F32)
                s8 = sbt(sa, "s8", [128, 8], F32); rn8 = sbt(sa, "rn8", [128, 8], F32); rk8 = sbt(sa, "rk8", [128, 8], F32)
                bkz = sbt(sa, "bkz", [128, 4, 6, 128], F32)
                kr_st = sbt(sa, "kr_st", [128, 4, 129, 4], F32)
                w_st = sbt(sa, "w_st", [128, 4, 128], F32)
                K.op("pool", lambda e: e.memset(bkz[:], 0.0), w=[bkz])
                K.op("pool", lambda e: e.memset(kr_st[:], 0.0), w=[kr_st])
                K.op("pool", lambda e: e.memset(u[:], 0.0), w=[u])
                K.op("pool", lambda e: e.memset(xnT[1][:], 0.0), w=[xnT[1]])

                K.dma("sp", xt[0][:], x[0:128, :], w=[xt[0]])
                for i in range(NT):
                    t0 = i * 128
                    xb = xt[i % 2]; xT = xnT[i % 2]; xTp = xnT[(i + 1) % 2]
                    if i + 1 < NT:
                        K.dma("sp", xt[(i + 1) % 2][:], x[t0 + 128:t0 + 256, :], w=[xt[(i + 1) % 2]])
                    rms_rstd(xb, ss, rstd, junk)
                    K.op("act", lambda e: e.activation(out=xn[:], in_=xb[:], func=AF.Copy, scale=rstd[:, 0:1]),
                         r=[xb, rstd], w=[xn])
                    tp = K.ps()
                    tpb = tp[:].bitcast(BF16)
                    for kc in range(8):
                        K.op("pe", lambda e: e.transpose(out=tpb[:, kc * 128:(kc + 1) * 128], in_=xn[:, kc * 128:(kc + 1) * 128],
                                                         identity=identb[:]), r=[xn, identb], w=[tp], inc=(kc == 7))
                    K.op("dve", lambda e: e.tensor_copy(out=xT[:, :, 0:1], in_=xTp[:, :, 128:129]), r=[xTp], w=[xT])
                    K.op("dve", lambda e: e.tensor_copy(out=xT[:, :, 1:129],
                                                        in_=tpb[:, 0:1024].rearrange("p (c t) -> p c t", c=8)),
                         r=[tp], w=[xT])

                    def fm_group(W, col0, nch, ps):
                        for j in range(nch):
                            for kc in range(8):
                                K.op("pe", lambda e: e.matmul(out=ps[:, j * 128:(j + 1) * 128],
                                                              lhsT=W[:, kc, col0 + j * 128: col0 + (j + 1) * 128],
                                                              rhs=xT[:, kc, 1:129], start=(kc == 0), stop=(kc == 7)),
                                     r=[W, xT], w=[ps], inc=(kc == 7 and j == nch - 1))

                    pA = K.ps(); fm_group(Wc, 0, 4, pA)
                    K.op("act", lambda e: e.activation(out=gbs[:], in_=pA[:].rearrange("p (c t) -> p c t", c=4), func=AF.Copy),
                         r=[pA], w=[gbs])
                    pB = K.ps(); fm_group(Wc, 512, 4, pB)
                    K.op("act", lambda e: e.activation(out=gcs[:], in_=pB[:].rearrange("p (c t) -> p c t", c=4), func=AF.Copy),
                         r=[pB], w=[gcs])
                    pC = K.ps(); fm_group(Wc, 1024, 4, pC)
                    K.op("dve", lambda e: e.tensor_copy(out=u[:, :, 0:2], in_=u[:, :, 128:130]), r=[u], w=[u])
                    K.op("dve", lambda e: e.tensor_tensor(out=u[:, :, 2:130], in0=gcs[:],
                                                          in1=pC[:].rearrange("p (c t) -> p c t", c=4), op=ALU.mult),
                         r=[gcs, pC, u], w=[u])
                    for cc in range(4):
                        K.op("dve", lambda e: e.tensor_scalar(out=yt[:, cc, :], in0=u[:, cc, 2:130], scalar1=cw[:, 2, cc:cc + 1],
                                                              scalar2=None, op0=ALU.mult), r=[u, cw], w=[yt])
                        K.op("dve", lambda e: e.scalar_tensor_tensor(out=yt[:, cc, :], in0=u[:, cc, 1:129], scalar=cw[:, 1, cc:cc + 1],
                                                                     in1=yt[:, cc, :], op0=ALU.mult, op1=ALU.add),
                             r=[u, cw, yt], w=[yt])
                        K.op("dve", lambda e: e.scalar_tensor_tensor(out=yt[:, cc, :], in0=u[:, cc, 0:128], scalar=cw[:, 0, cc:cc + 1],
                                                                     in1=yt[:, cc, :], op0=ALU.mult, op1=ALU.add),
                             r=[u, cw, yt], w=[yt])
                        K.op("dve", lambda e: e.scalar_tensor_tensor(out=yc[:, cc, :], in0=yt[:, cc, :], scalar=cb[:, cc:cc + 1],
                                                                     in1=gbs[:, cc, :], op0=ALU.add, op1=ALU.mult),
                             r=[yt, cb, gbs], w=[yc])
                    for hh in range(2):
                        pY = K.ps()
                        for j in range(4):
                            fc = hh * 4 + j
                            for kc in range(4):
                                K.op("pe", lambda e: e.matmul(out=pY[:, j * 128:(j + 1) * 128], lhsT=wa[:, kc, fc * 128:(fc + 1) * 128],
                                                              rhs=yc[:, kc, :], start=(kc == 0), stop=(kc == 3)),
                                     r=[wa, yc], w=[pY], inc=(kc == 3 and j == 3))
                        pG = K.ps(); fm_group(Wg, hh * 512, 4, pG)
                        K.op("act", lambda e: e.activation(out=sga[:], in_=pG[:].rearrange("p (c t) -> p c t", c=4), func=AF.Sigmoid),
                             r=[pG], w=[sga])
                        K.op("dve", lambda e: e.tensor_tensor(out=pa_st[:, hh * 4:(hh + 1) * 4, :], in0=sga[:],
                                                              in1=pY[:].rearrange("p (c t) -> p c t", c=4), op=ALU.mult),
                             r=[sga, pY, pa_st], w=[pa_st])
                        pG2 = K.ps(); fm_group(Wg, 1024 + hh * 512, 4, pG2)
                        K.op("act", lambda e: e.activation(out=sgb_st[:, hh * 4:(hh + 1) * 4, :],
                                                           in_=pG2[:].rearrange("p (c t) -> p c t", c=4), func=AF.Sigmoid),
                             r=[pG2, sgb_st], w=[sgb_st])
                    K.dma("sp", s_pa.rearrange("(c p) t -> p c t", p=128)[:, :, t0:t0 + 128], pa_st[:], r=[pa_st])
                    K.dma("sp", s_sgb.rearrange("(c p) t -> p c t", p=128)[:, :, t0:t0 + 128], sgb_st[:], r=[sgb_st])

                    def tm_group(col0, ps):
                        for kc in range(8):
                            K.op("pe", lambda e: e.matmul(out=ps[:], lhsT=xT[:, kc, 1:129], rhs=W1[:, kc, col0:col0 + 512],
                                                          start=(kc == 0), stop=False), r=[W1, xT], w=[ps], inc=False)
                        for kc in range(8):
                            K.op("pe", lambda e: e.matmul(out=ps[:], lhsT=xT[:, kc, 0:128], rhs=W2[:, kc, col0:col0 + 512],
                                                          start=False, stop=(kc == 7)), r=[W2, xT], w=[ps], inc=(kc == 7))
                    pS = K.ps()
                    for j in range(2):
                        c0 = 1536 + j * 128
                        for kc in range(8):
                            K.op("pe", lambda e: e.matmul(out=pS[:, j * 128:(j + 1) * 128], lhsT=W1[:, kc, c0:c0 + 128],
                                                          rhs=xT[:, kc, 1:129], start=(kc == 0), stop=False),
                                 r=[W1, xT], w=[pS], inc=False)
                        for kc in range(8):
                            K.op("pe", lambda e: e.matmul(out=pS[:, j * 128:(j + 1) * 128], lhsT=W2[:, kc, c0:c0 + 128],
                                                          rhs=xT[:, kc, 0:128], start=False, stop=(kc == 7)),
                                 r=[W2, xT], w=[pS], inc=(kc == 7 and j == 1))
                    K.op("act", lambda e: e.activation(out=waT[0:64, :], in_=pS[0:64, 0:128], func=AF.Tanh), r=[pS], w=[waT])
                    K.op("act", lambda e: e.activation(out=waT[64:128, :], in_=pS[64:128, 0:128], func=AF.Copy), r=[pS, waT], w=[waT])
                    K.op("act", lambda e: e.activation(out=sgd[:], in_=pS[:, 128:256], func=AF.Sigmoid), r=[pS], w=[sgd])
                    pD = K.ps()
                    K.op("pe", lambda e: e.matmul(out=pD[:], lhsT=waT[0:64, :], rhs=waup[0:64, :], start=True, stop=True),
                         r=[waT, waup], w=[pD])
                    K.op("dve", lambda e: e.tensor_tensor(out=t_w[:], in0=pD[:], in1=w0_b[:], op=ALU.add), r=[pD, w0_b], w=[t_w])
                    K.op("act", lambda e: e.activation(out=t_w[:], in_=t_w[:], func=AF.Sigmoid), r=[t_w], w=[t_w])
                    K.op("act", lambda e: e.activation(out=t_w[:], in_=t_w[:], func=AF.Exp, scale=-0.6065306597126334),
                         r=[t_w], w=[t_w])
                    pAa = K.ps()
                    K.op("pe", lambda e: e.matmul(out=pAa[:], lhsT=waT[64:128, :], rhs=waup[64:128, :], start=True, stop=True),
                         r=[waT, waup], w=[pAa])
                    K.op("dve", lambda e: e.tensor_tensor(out=t_a[:], in0=pAa[:], in1=a0_b[:], op=ALU.add), r=[pAa, a0_b], w=[t_a])
                    K.op("act", lambda e: e.activation(out=t_a[:], in_=t_a[:], func=AF.Sigmoid), r=[t_a], w=[t_a])
                    pGg = K.ps()
                    K.op("pe", lambda e: e.matmul(out=pGg[:], lhsT=sgd[:], rhs=gup[:], start=True, stop=True), r=[sgd, gup], w=[pGg])
                    K.op("act", lambda e: e.activation(out=t_g[:], in_=pGg[:], func=AF.Copy), r=[pGg], w=[t_g])
                    K.dma("sp", s_g[t0:t0 + 128, :], t_g[:], r=[t_g])
                    pR = K.ps(); tm_group(0, pR)
                    K.op("act", lambda e: e.activation(out=t_r[:], in_=pR[:], func=AF.Copy), r=[pR], w=[t_r])
                    pK = K.ps(); tm_group(512, pK)
                    K.op("dve", lambda e: e.tensor_tensor(out=t_kk[:], in0=pK[:], in1=kk_b[:], op=ALU.mult), r=[pK, kk_b], w=[t_kk])
                    K.op("act", lambda e: e.activation(out=t_sq[:], in_=t_kk[:], func=AF.Square), r=[t_kk], w=[t_sq])
                    K.op("dve", lambda e: e.tensor_reduce(out=s8[:], in_=t_sq[:].rearrange("p (h n) -> p h n", h=8), axis=AX.X, op=ALU.add),
                         r=[t_sq], w=[s8])
                    K.op("dve", lambda e: e.tensor_scalar(out=rn8[:], in0=s8[:], scalar1=1e-12, scalar2=None, op0=ALU.add),
                         r=[s8], w=[rn8])
                    K.op("act", lambda e: e.activation(out=rn8[:], in_=rn8[:], func=AF.Sqrt), r=[rn8], w=[rn8])
                    K.op("dve", lambda e: e.reciprocal(out=rn8[:], in_=rn8[:]), r=[rn8], w=[rn8])
                    K.op("dve", lambda e: e.tensor_tensor(out=t_kk[:].rearrange("p (h n) -> p h n", h=8),
                                                          in0=t_kk[:].rearrange("p (h n) -> p h n", h=8),
                                                          in1=rn8[:].unsqueeze(2).to_broadcast([128, 8, 64]), op=ALU.mult),
                         r=[t_kk, rn8], w=[t_kk])
                    K.op("dve", lambda e: e.scalar_tensor_tensor(out=t_3[:], in0=t_a[:], scalar=-1.0, in1=ka_b[:], op0=ALU.add, op1=ALU.mult),
                         r=[t_a, ka_b], w=[t_3])
                    K.op("dve", lambda e: e.scalar_tensor_tensor(out=t_km[:], in0=t_3[:], scalar=1.0, in1=pK[:], op0=ALU.add, op1=ALU.mult),
                         r=[t_3, pK], w=[t_km])
                    K.op("pool", lambda e: e.tensor_tensor(out=t_4[:], in0=t_r[:], in1=t_km[:], op=ALU.mult), r=[t_r, t_km], w=[t_4])
                    K.op("pool", lambda e: e.tensor_tensor(out=t_4[:], in0=t_4[:], in1=rk_b[:], op=ALU.mult), r=[t_4, rk_b], w=[t_4])
                    K.op("dve", lambda e: e.tensor_reduce(out=rk8[:], in_=t_4[:].rearrange("p (h n) -> p h n", h=8), axis=AX.X, op=ALU.add),
                         r=[t_4], w=[rk8])
                    K.dma("sp", s_rk[t0:t0 + 128, :], rk8[:], r=[rk8])
                    kk5 = t_kk[:].rearrange("p (q j n) -> p q j n", q=4, j=2)
                    a5 = t_a[:].rearrange("p (q j n) -> p q j n", q=4, j=2)
                    km5 = t_km[:].rearrange("p (q j n) -> p q j n", q=4, j=2)
                    for j in range(2):
                        K.op("dve", lambda e: e.scalar_tensor_tensor(out=bkz[:, :, j, j * 64:(j + 1) * 64], in0=kk5[:, :, j, :], scalar=-1.0,
                                                                     in1=a5[:, :, j, :], op0=ALU.mult, op1=ALU.mult),
                             r=[t_kk, t_a, bkz], w=[bkz])
                        K.op("pool", lambda e: e.tensor_copy(out=bkz[:, :, 4 + j, j * 64:(j + 1) * 64], in_=km5[:, :, j, :]),
                             r=[t_km, bkz], w=[bkz])
                    K.dma("sp", s_bk.rearrange("q j t c -> t q j c")[t0:t0 + 128], bkz[:], r=[bkz])
                    pV = K.ps(); tm_group(1024, pV)
                    K.op("act", lambda e: e.activation(out=t_v[:], in_=pV[:], func=AF.Copy), r=[pV], w=[t_v])
                    K.dma("sp", s_v[t0:t0 + 128, :], t_v[:], r=[t_v])
                    K.op("pool", lambda e: e.tensor_copy(out=kr_st[:, :, 0, 2:4], in_=kr_st[:, :, 128, 2:4]), r=[kr_st], w=[kr_st])
                    for (src, kind) in ((t_kk, 0), (t_r, 1), (t_w, 2)):
                        pT = K.ps()
                        for c in range(4):
                            K.op("pe", lambda e: e.transpose(out=pT[:, c * 128:(c + 1) * 128], in_=src[:, c * 128:(c + 1) * 128],
                                                             identity=ident[:]), r=[src, ident], w=[pT], inc=(c == 3))
                        pT3 = pT[:].rearrange("p (c t) -> p c t", c=4)
                        if kind == 0:
                            for j in range(2):
                                K.op("act", lambda e: e.activation(out=kr_st[:, :, 0:128, j], in_=pT3, func=AF.Copy, scale=masks[:, j:j + 1]),
                                     r=[pT, masks, kr_st], w=[kr_st])
                        elif kind == 1:
                            for j in range(2):
                                K.op("act", lambda e: e.activation(out=kr_st[:, :, 1:129, 2 + j], in_=pT3, func=AF.Copy, scale=masks[:, j:j + 1]),
                                     r=[pT, masks, kr_st], w=[kr_st])
                        else:
                            K.op("act", lambda e: e.activation(out=w_st[:], in_=pT3, func=AF.Copy), r=[pT], w=[w_st])
                    K.dma("sp", s_kr.rearrange("(c p) t j -> p c t j", p=128)[:, :, t0:t0 + 128, :], kr_st[:, :, 0:128, :], r=[kr_st])
                    if i == NT - 1:
                        K.dma("sp", s_kr.rearrange("(c p) t j -> p c t j", p=128)[:, :, T:T + 1, :], kr_st[:, :, 128:129, :], r=[kr_st])
                    K.dma("sp", s_w.rearrange("(c p) t -> p c t", p=128)[:, :, t0:t0 + 128], w_st[:], r=[w_st])
            K.barrier()

        if "B" in phases:
            with ExitStack() as sb_:
                SUB = 32
                Tst = [sbt(sb_, f"Tst{q}", [128, 64], F32) for q in range(4)]
                KRt = [[sbt(sb_, f"KRt{q}_{b}", [128, 128, 4], F32) for b in range(2)] for q in range(4)]
                KRl = [sbt(sb_, f"KRl{q}", [128, 1, 4], F32) for q in range(4)]
                Yl = [sbt(sb_, f"Yl{q}", [4, 64], F32) for q in range(4)]
                Wt = [[sbt(sb_, f"Wt{q}_{b}", [128, 128], F32) for b in range(2)] for q in range(4)]
                BKh = [[sbt(sb_, f"BK{h}_{b}", [64, SUB, 128], F32) for b in range(2)] for h in range(2)]
                SVh = [[sbt(sb_, f"SV{h}_{b}", [64, SUB, 64], F32) for b in range(2)] for h in range(2)]
                BKr = [[BKh[q // 2][b].sub() for b in range(2)] for q in range(4)]
                SVr = [[SVh[q // 2][b].sub() for b in range(2)] for q in range(4)]
                psq = [K.ps() for _ in range(4)]
                skk_r = [psq[q].sub() for q in range(4)]; U_r = [psq[q].sub() for q in range(4)]
                for q in range(4):
                    K.op("dve", lambda e: e.memset(Tst[q][:], 0.0), w=[Tst[q]])
                NSUB = 128 // SUB
                for c in range(NT):
                    t0 = c * 128; par = c % 2
                    for q in range(4):
                        K.dma("sp", KRt[q][par][:], s_kr[q * 128:(q + 1) * 128, t0:t0 + 128, :], w=[KRt[q][par]])
                        K.dma("sp", Wt[q][par][:], s_w[q * 128:(q + 1) * 128, t0:t0 + 128], w=[Wt[q][par]])
                    for sc in range(NSUB):
                        sg = c * NSUB + sc; sp2 = sg % 2; ts0 = t0 + sc * SUB
                        for q in range(4):
                            base = 32 * (q % 2)
                            K.dma("sp", BKh[q // 2][sp2][base:base + 6, :, :], s_bk[q, :, ts0:ts0 + SUB, :], w=[BKr[q][sp2]])
                            K.dma("sp", SVh[q // 2][sp2][base + 4:base + 6, :, :],
                                  s_v[ts0:ts0 + SUB, q * 128:(q + 1) * 128].rearrange("t (j n) -> j t n", j=2), w=[SVr[q][sp2]])
                        for ts in range(SUB):
                            tl = sc * SUB + ts
                            for q in range(4):
                                K.op("pe", lambda e: e.matmul(out=psq[q][0:4, 0:64], lhsT=KRt[q][par][:, tl, :], rhs=Tst[q][:], start=True, stop=True),
                                     r=[KRt[q][par], Tst[q]], w=[skk_r[q]])
                            for q in range(4):
                                base = 32 * (q % 2); SV = SVh[q // 2][sp2]
                                K.op("act", lambda e: e.activation(out=SV[base:base + 4, ts, :], in_=psq[q][0:4, 0:64], func=AF.Copy),
                                     r=[skk_r[q]], w=[SVr[q][sp2]])
                            for q in range(4):
                                base = 32 * (q % 2); BK = BKh[q // 2][sp2]; SV = SVh[q // 2][sp2]
                                K.op("pe", lambda e: e.matmul(out=psq[q][:, 64:128], lhsT=BK[base:base + 6, ts, :], rhs=SV[base:base + 6, ts, :],
                                                              start=True, stop=True), r=[BKr[q][sp2], SVr[q][sp2]], w=[U_r[q]])
                            for q in range(4):
                                K.op("dve", lambda e: e.scalar_tensor_tensor(out=Tst[q][:], in0=Tst[q][:], scalar=Wt[q][par][:, tl:tl + 1],
                                                                             in1=psq[q][:, 64:128], op0=ALU.mult, op1=ALU.add),
                                     r=[U_r[q], Wt[q][par], Tst[q]], w=[Tst[q]])
                        for q in range(4):
                            base = 32 * (q % 2); SV = SVh[q // 2][sp2]
                            s0 = 1 if sg == 0 else 0
                            K.dma("sp", s_y[ts0 - 1 + s0:ts0 + SUB - 1, q * 128:(q + 1) * 128].rearrange("t (j n) -> j t n", j=2),
                                  SV[base + 2:base + 4, s0:SUB, :], r=[SVr[q][sp2]])
                for q in range(4):
                    K.dma("sp", KRl[q][:], s_kr[q * 128:(q + 1) * 128, T:T + 1, :], w=[KRl[q]])
                for q in range(4):
                    K.op("pe", lambda e: e.matmul(out=psq[q][0:4, 0:64], lhsT=KRl[q][:, 0, :], rhs=Tst[q][:], start=True, stop=True),
                         r=[KRl[q], Tst[q]], w=[skk_r[q]])
                    K.op("act", lambda e: e.activation(out=Yl[q][:], in_=psq[q][0:4, 0:64], func=AF.Copy), r=[skk_r[q]], w=[Yl[q]])
                    K.dma("sp", s_y[T - 1:T, q * 128:(q + 1) * 128].rearrange("t (j n) -> j t n", j=2), Yl[q][2:4, :].unsqueeze(1), r=[Yl[q]])
            K.barrier()

        if "1" in phases:
            with ExitStack() as sc_:
                wb = sbt(sc_, "wb", [128, 4, 1024], BF16); wo = sbt(sc_, "wo", [128, 8, 1024], BF16)
                lng_b = bcast(sc_, "lng_b", ln_g, 512); lnb_b = bcast(sc_, "lnb_b", ln_b, 512)
                with ExitStack() as sp_:
                    load_cast(sp_, wb, w_b, 4, 1024)
                    load_cast(sp_, wo, w_out, 8, 1024)
                K.barrier()
                y_t = sbt(sc_, "y_t", [128, 512], F32); v_t = sbt(sc_, "v_tC", [128, 512], F32)
                g_t = sbt(sc_, "g_tC", [128, 512], F32); rk_t = sbt(sc_, "rk_tC", [128, 8], F32)
                m8 = sbt(sc_, "m8", [128, 8], F32); v8 = sbt(sc_, "v8", [128, 8], F32)
                c_t = sbt(sc_, "c_t", [128, 512], F32); sq_t = sbt(sc_, "sq_t", [128, 512], F32)
                mix = sbt(sc_, "mixC", [128, 512], BF16); mixT = sbt(sc_, "mixT", [128, 4, 128], BF16)
                pa_t = sbt(sc_, "pa_tC", [128, 8, 128], BF16); sgb_t = sbt(sc_, "sgb_tC", [128, 8, 128], BF16)
                mg = sbt(sc_, "mgC", [128, 8, 128], F32); mgb = sbt(sc_, "mgbC", [128, 8, 128], BF16)
                x_t = sbt(sc_, "x_tC", [128, 1024], F32); h1 = sbt(sc_, "h1C", [128, 1024], F32)
                for i in range(NT):
                    t0 = i * 128
                    K.dma("sp", y_t[:], s_y[t0:t0 + 128, :], w=[y_t])
                    K.dma("sp", v_t[:], s_v[t0:t0 + 128, :], w=[v_t])
                    K.dma("sp", g_t[:], s_g[t0:t0 + 128, :], w=[g_t])
                    K.dma("sp", rk_t[:], s_rk[t0:t0 + 128, :], w=[rk_t])
                    K.dma("sp", pa_t[:], s_pa.rearrange("(c p) t -> p c t", p=128)[:, :, t0:t0 + 128], w=[pa_t])
                    K.dma("sp", sgb_t[:], s_sgb.rearrange("(c p) t -> p c t", p=128)[:, :, t0:t0 + 128], w=[sgb_t])
                    K.dma("sp", x_t[:], x[t0:t0 + 128, :], w=[x_t])
                    y3 = y_t[:].rearrange("p (h n) -> p h n", h=8)
                    c3 = c_t[:].rearrange("p (h n) -> p h n", h=8)
                    K.op("dve", lambda e: e.tensor_reduce(out=m8[:], in_=y3, axis=AX.X, op=ALU.add), r=[y_t], w=[m8])
                    K.op("dve", lambda e: e.tensor_scalar(out=m8[:], in0=m8[:], scalar1=1.0 / 64.0, scalar2=None, op0=ALU.mult), r=[m8], w=[m8])
                    K.op("dve", lambda e: e.tensor_tensor(out=c3, in0=y3, in1=m8[:].unsqueeze(2).to_broadcast([128, 8, 64]), op=ALU.subtract),
                         r=[y_t, m8], w=[c_t])
                    K.op("act", lambda e: e.activation(out=sq_t[:], in_=c_t[:], func=AF.Square), r=[c_t], w=[sq_t])
                    K.op("dve", lambda e: e.tensor_reduce(out=v8[:], in_=sq_t[:].rearrange("p (h n) -> p h n", h=8), axis=AX.X, op=ALU.add),
                         r=[sq_t], w=[v8])
                    K.op("dve", lambda e: e.tensor_scalar(out=v8[:], in0=v8[:], scalar1=1.0 / 64.0, scalar2=64e-5, op0=ALU.mult, op1=ALU.add),
                         r=[v8], w=[v8])
                    K.op("act", lambda e: e.activation(out=v8[:], in_=v8[:], func=AF.Sqrt), r=[v8], w=[v8])
                    K.op("dve", lambda e: e.reciprocal(out=v8[:], in_=v8[:]), r=[v8], w=[v8])
                    K.op("dve", lambda e: e.tensor_tensor(out=c3, in0=c3, in1=v8[:].unsqueeze(2).to_broadcast([128, 8, 64]), op=ALU.mult),
                         r=[c_t, v8], w=[c_t])
                    K.op("dve", lambda e: e.tensor_tensor(out=c_t[:], in0=c_t[:], in1=lng_b[:], op=ALU.mult), r=[c_t, lng_b], w=[c_t])
                    K.op("dve", lambda e: e.tensor_tensor(out=c_t[:], in0=c_t[:], in1=lnb_b[:], op=ALU.add), r=[c_t, lnb_b], w=[c_t])
                    K.op("pool", lambda e: e.tensor_tensor(out=sq_t[:].rearrange("p (h n) -> p h n", h=8),
                                                           in0=v_t[:].rearrange("p (h n) -> p h n", h=8),
                                                           in1=rk_t[:].unsqueeze(2).to_broadcast([128, 8, 64]), op=ALU.mult),
                         r=[v_t, rk_t, sq_t], w=[sq_t])
                    K.op("dve", lambda e: e.tensor_tensor(out=c_t[:], in0=c_t[:], in1=sq_t[:], op=ALU.add), r=[c_t, sq_t], w=[c_t])
                    K.op("dve", lambda e: e.tensor_tensor(out=mix[:], in0=c_t[:], in1=g_t[:], op=ALU.mult), r=[c_t, g_t], w=[mix])
                    pM = K.ps(); pMb = pM[:].bitcast(BF16)
                    for c in range(4):
                        K.op("pe", lambda e: e.transpose(out=pMb[:, c * 128:(c + 1) * 128], in_=mix[:, c * 128:(c + 1) * 128],
                                                         identity=identb[:]), r=[mix, identb], w=[pM], inc=(c == 3))
                    K.op("act", lambda e: e.activation(out=mixT[:], in_=pMb[:, 0:512].rearrange("p (c t) -> p c t", c=4), func=AF.Copy),
                         r=[pM], w=[mixT])
                    for hh in range(2):
                        pYb = K.ps()
                        for j in range(4):
                            fc = hh * 4 + j
                            for kc in range(4):
                                K.op("pe", lambda e: e.matmul(out=pYb[:, j * 128:(j + 1) * 128], lhsT=wb[:, kc, fc * 128:(fc + 1) * 128],
                                                              rhs=mixT[:, kc, :], start=(kc == 0), stop=(kc == 3)),
                                     r=[wb, mixT], w=[pYb], inc=(kc == 3 and j == 3))
                        K.op("dve", lambda e: e.tensor_tensor(out=mg[:, hh * 4:(hh + 1) * 4, :], in0=sgb_t[:, hh * 4:(hh + 1) * 4, :],
                                                              in1=pYb[:].rearrange("p (c t) -> p c t", c=4), op=ALU.mult),
                             r=[sgb_t, pYb, mg], w=[mg])
                    K.op("dve", lambda e: e.tensor_tensor(out=mgb[:], in0=mg[:], in1=pa_t[:], op=ALU.add), r=[mg, pa_t], w=[mgb])
                    for hh in range(2):
                        pH = K.ps()
                        for kc in range(8):
                            K.op("pe", lambda e: e.matmul(out=pH[:], lhsT=mgb[:, kc, :], rhs=wo[:, kc, hh * 512:(hh + 1) * 512],
                                                          start=(kc == 0), stop=(kc == 7)), r=[mgb, wo], w=[pH], inc=(kc == 7))
                        K.op("dve", lambda e: e.tensor_tensor(out=h1[:, hh * 512:(hh + 1) * 512], in0=pH[:], in1=x_t[:, hh * 512:(hh + 1) * 512],
                                                              op=ALU.add), r=[pH, x_t, h1], w=[h1])
                    K.dma("sp", s_h1[t0:t0 + 128, :], h1[:], r=[h1])
            K.barrier()

        if "2" in phases:
            with ExitStack() as sd:
                wqs = sbt(sd, "wqs", [128, 8, 2048], F32)
                skt = sbt(sd, "skt", [128, 16, 128], F32)
                pg = sbt(sd, "pg", [128, 8, 1024], BF16); pp = sbt(sd, "pp", [128, 2, 1024], BF16)
                gffn_b = bcast(sd, "gffn_b", gffn, 1024); gfin_b = bcast(sd, "gfin_b", gfin, 1024)
                iota_b = bcast(sd, "iota_b", iota_d, 256)
                gp = colvec(sd, "gp", gple, 8)
                for kc in range(8):
                    K.dma("sp", wqs[:, kc, :], wq[kc * 128:(kc + 1) * 128, :], w=[wqs])
                K.dma("sp", skt[:], skT.rearrange("s d n -> d s n"), w=[skt])
                with ExitStack() as sp_:
                    load_cast(sp_, pg, pgw, 8, 1024, scale_tile=gp)
                    load_cast(sp_, pp, ppw, 2, 1024)
                K.barrier()
                NB = 8
                h1 = sbt(sd, "h1D", [128, 1024], F32); junk = sbt(sd, "junkD", [128, 1024], BF16)
                ss = sbt(sd, "ssD", [128, 1], F32); rstd = sbt(sd, "rstdD", [128, 1], F32)
                xg = sbt(sd, "xgD", [128, 1024], F32); xgT = sbt(sd, "xgT", [128, 8, 128], F32)
                q_sb = sbt(sd, "q_sb", [128, 16, 128], F32); s_sb = sbt(sd, "s_sb", [128, 16, 128], F32)
                tmp128 = sbt(sd, "tmp128", [128, 128], F32)
                sv = sbt(sd, "svD", [128, 16, 16], F32); si = sbt(sd, "siD", [128, 16, 16], U32); sif = sbt(sd, "sifD", [128, 16, 16], F32)
                cand = sbt(sd, "cand", [128, 8, 256], F32); tmp256 = sbt(sd, "tmp256", [128, 256], F32)
                tops = sbt(sd, "tops", [128, 8, 16], F32); pos = sbt(sd, "posD", [128, 8, 16], U32)
                pi_u = sbt(sd, "pi_u", [128, 128], U32); pj_u = sbt(sd, "pj_u", [128, 128], U32)
                pi_f = sbt(sd, "pi_f", [128, 128], F32); pj_f = sbt(sd, "pj_f", [128, 128], F32)
                ei = sbt(sd, "eiD", [128, 128], F32); ej = sbt(sd, "ejD", [128, 128], F32)
                idxf = sbt(sd, "idxf", [128, 128], F32); idx = sbt(sd, "idxD", [128, 128], I32)
                gate = sbt(sd, "gateD", [128, 8, 16], F32); gs8 = sbt(sd, "gs8", [128, 8], F32)
                hid = sbt(sd, "hidD", [128, 128], F32); h_x2 = sbt(sd, "h_x2", [128, 128], F32); h_sg = sbt(sd, "h_sg", [128, 128], F32)
                wts = sbt(sd, "wtsD", [128, 128], F32)
                Ug = [sbt(sd, f"Ug{b}", [128, 1024], F32) for b in range(NB)]
                Vg = Ug
                Vs = [sbt(sd, f"Vs{b}", [128, 1024], BF16) for b in range(2)]
                ttj = sbt(sd, "ttj", [128, 1024], F32)
                ttj2 = [ttj, sbt(sd, "ttjb", [128, 1024], F32)]
                h2 = sbt(sd, "h2D", [128, 1024], F32); xn3 = sbt(sd, "xn3", [128, 1024], BF16); xn3T = sbt(sd, "xn3T", [128, 8, 128], BF16)
                p_t = sbt(sd, "p_tD", [128, 256], F32); p_b = sbt(sd, "p_bD", [128, 256], BF16); pT = sbt(sd, "pTD", [128, 2, 128], BF16)
                sgp = xg; o_t = ttj
                for i in range(NT):
                    t0 = i * 128
                    K.dma("sp", h1[:], s_h1[t0:t0 + 128, :], w=[h1])
                    K.dma("sp", p_t[:], p_in[t0:t0 + 128, :], w=[p_t])
                    rms_rstd(h1, ss, rstd, junk)
                    K.op("dve", lambda e: e.scalar_tensor_tensor(out=xg[:], in0=h1[:], scalar=rstd[:, 0:1], in1=gffn_b[:],
                                                                 op0=ALU.mult, op1=ALU.mult), r=[h1, rstd, gffn_b], w=[xg])
                    for hh in range(2):
                        pX = K.ps()
                        for c in range(4):
                            kc = hh * 4 + c
                            K.op("pe", lambda e: e.transpose(out=pX[:, c * 128:(c + 1) * 128], in_=xg[:, kc * 128:(kc + 1) * 128],
                                                             identity=ident[:]), r=[xg, ident], w=[pX], inc=(c == 3))
                        K.op("act", lambda e: e.activation(out=xgT[:, hh * 4:(hh + 1) * 4, :], in_=pX[:].rearrange("p (c t) -> p c t", c=4),
                                                           func=AF.Copy), r=[pX, xgT], w=[xgT])
                    for g4 in range(4):
                        pQ = K.ps()
                        for j in range(4):
                            ch = g4 * 4 + j
                            for kc in range(8):
                                K.op("pe", lambda e: e.matmul(out=pQ[:, j * 128:(j + 1) * 128], lhsT=wqs[:, kc, ch * 128:(ch + 1) * 128],
                                                              rhs=xgT[:, kc, :], start=(kc == 0), stop=(kc == 7)),
                                     r=[wqs, xgT], w=[pQ], inc=(kc == 7 and j == 3))
                        K.op("act", lambda e: e.activation(out=q_sb[:, g4 * 4:(g4 + 1) * 4, :], in_=pQ[:].rearrange("p (c t) -> p c t", c=4),
                                                           func=AF.Copy), r=[pQ, q_sb], w=[q_sb])
                    for g4 in range(4):
                        pS2 = K.ps()
                        for j in range(4):
                            ch = g4 * 4 + j
                            K.op("pe", lambda e: e.matmul(out=pS2[:, j * 128:(j + 1) * 128], lhsT=q_sb[:, ch, :], rhs=skt[:, ch, :],
                                                          start=True, stop=True), r=[q_sb, skt], w=[pS2], inc=(j == 3))
                        K.op("act", lambda e: e.activation(out=s_sb[:, g4 * 4:(g4 + 1) * 4, :], in_=pS2[:].rearrange("p (c t) -> p c t", c=4),
                                                           func=AF.Copy), r=[pS2, s_sb], w=[s_sb])
                    for st_ in range(16):
                        K.op("dve", lambda e: e.max(out=sv[:, st_, 0:8], in_=s_sb[:, st_, :]), r=[s_sb, sv], w=[sv])
                        K.op("dve", lambda e: e.match_replace(out=tmp128[:], in_to_replace=sv[:, st_, 0:8], in_values=s_sb[:, st_, :],
                                                              imm_value=-1e30), r=[sv, s_sb], w=[tmp128])
                        K.op("dve", lambda e: e.max(out=sv[:, st_, 8:16], in_=tmp128[:]), r=[tmp128, sv], w=[sv])
                        K.op("dve", lambda e: e.max_index(out=si[:, st_, 0:8], in_max=sv[:, st_, 0:8], in_values=s_sb[:, st_, :]),
                             r=[sv, s_sb, si], w=[si])
                        K.op("dve", lambda e: e.max_index(out=si[:, st_, 8:16], in_max=sv[:, st_, 8:16], in_values=s_sb[:, st_, :]),
                             r=[sv, s_sb, si], w=[si])
                    K.op("dve", lambda e: e.tensor_copy(out=sif[:], in_=si[:]), r=[si], w=[sif])
                    sv5 = sv[:].rearrange("p (h c) k -> p h c k", c=2)
                    K.op("dve", lambda e: e.tensor_tensor(out=cand[:].rearrange("p h (i j) -> p h i j", i=16),
                                                          in0=sv5[:, :, 0, :].unsqueeze(3).to_broadcast([128, 8, 16, 16]),
                                                          in1=sv5[:, :, 1, :].unsqueeze(2).to_broadcast([128, 8, 16, 16]), op=ALU.add),
                         r=[sv], w=[cand])
                    for h in range(8):
                        K.op("dve", lambda e: e.max(out=tops[:, h, 0:8], in_=cand[:, h, :]), r=[cand, tops], w=[tops])
                        K.op("dve", lambda e: e.match_replace(out=tmp256[:], in_to_replace=tops[:, h, 0:8], in_values=cand[:, h, :],
                                                              imm_value=-1e30), r=[tops, cand], w=[tmp256])
                        K.op("dve", lambda e: e.max(out=tops[:, h, 8:16], in_=tmp256[:]), r=[tmp256, tops], w=[tops])
                        K.op("dve", lambda e: e.max_index(out=pos[:, h, 0:8], in_max=tops[:, h, 0:8], in_values=cand[:, h, :]),
                             r=[tops, cand, pos], w=[pos])
                        K.op("dve", lambda e: e.max_index(out=pos[:, h, 8:16], in_max=tops[:, h, 8:16], in_values=cand[:, h, :]),
                             r=[tops, cand, pos], w=[pos])
                    posf = pos[:].rearrange("p h k -> p (h k)")
                    K.op("dve", lambda e: e.tensor_single_scalar(out=pi_u[:], in_=posf, scalar=4, op=ALU.logical_shift_right), r=[pos], w=[pi_u])
                    K.op("dve", lambda e: e.tensor_single_scalar(out=pj_u[:], in_=posf, scalar=15, op=ALU.bitwise_and), r=[pos], w=[pj_u])
                    K.op("dve", lambda e: e.tensor_copy(out=pi_f[:], in_=pi_u[:]), r=[pi_u], w=[pi_f])
                    K.op("dve", lambda e: e.tensor_copy(out=pj_f[:], in_=pj_u[:]), r=[pj_u], w=[pj_f])
                    sif5 = sif[:].rearrange("p (h c) k -> p h c k", c=2)
                    OH4 = q_sb[:].rearrange("p c t -> p (c t)").rearrange("p (h k i) -> p h k i", h=8, k=16)
                    io4 = iota_b[:].rearrange("p (k i) -> p k i", k=16).unsqueeze(1).to_broadcast([128, 8, 16, 16])
                    for (pf, cidx, eo) in ((pi_f, 0, ei), (pj_f, 1, ej)):
                        K.op("dve", lambda e: e.tensor_tensor(out=OH4, in0=io4,
                                                              in1=pf[:].rearrange("p (h k) -> p h k", h=8).unsqueeze(3).to_broadcast([128, 8, 16, 16]),
                                                              op=ALU.is_equal), r=[iota_b, pf, q_sb], w=[q_sb])
                        K.op("dve", lambda e: e.tensor_tensor(out=OH4, in0=OH4,
                                                              in1=sif5[:, :, cidx, :].unsqueeze(2).to_broadcast([128, 8, 16, 16]), op=ALU.mult),
                             r=[q_sb, sif], w=[q_sb])
                        K.op("dve", lambda e: e.tensor_reduce(out=eo[:], in_=OH4.rearrange("p h k i -> p (h k) i"), axis=AX.X, op=ALU.add),
                             r=[q_sb], w=[eo])
                    K.op("dve", lambda e: e.scalar_tensor_tensor(out=idxf[:], in0=ei[:], scalar=128.0, in1=ej[:], op0=ALU.mult, op1=ALU.add),
                         r=[ei, ej], w=[idxf])
                    K.op("dve", lambda e: e.tensor_copy(out=idx[:], in_=idxf[:]), r=[idxf, idx], w=[idx])
                    K.op("dve", lambda e: e.tensor_tensor(out=gate[:], in0=tops[:], in1=tops[:, :, 0:1].to_broadcast([128, 8, 16]), op=ALU.subtract),
                         r=[tops, gate], w=[gate])
                    K.op("act", lambda e: e.activation(out=gate[:], in_=gate[:], func=AF.Exp), r=[gate], w=[gate])
                    K.op("dve", lambda e: e.tensor_reduce(out=gs8[:], in_=gate[:], axis=AX.X, op=ALU.add), r=[gate], w=[gs8])
                    K.op("dve", lambda e: e.reciprocal(out=gs8[:], in_=gs8[:]), r=[gs8], w=[gs8])
                    K.op("dve", lambda e: e.tensor_tensor(out=gate[:], in0=gate[:], in1=gs8[:].unsqueeze(2).to_broadcast([128, 8, 16]), op=ALU.mult),
                         r=[gate, gs8], w=[gate])
                    for sl in range(128):
                        ub = Ug[sl % NB]
                        K.dma("pool", None, None, r=[idx], w=[ub], fn=lambda e: e.indirect_dma_start(
                            out=ub[:], out_offset=None, in_=tab_u, in_offset=bass.IndirectOffsetOnAxis(ap=idx[:, sl:sl + 1], axis=0)))
                        tj = ttj2[sl % 2]
                        K.op("dve", lambda e: e.tensor_tensor(out=tj[:], in0=ub[:], in1=xg[:], op=ALU.mult), r=[ub, xg], w=[tj])
                        K.op("act", lambda e: e.activation(out=junk[:], in_=tj[:], func=AF.Copy, accum_out=hid[:, sl:sl + 1]),
                             r=[tj, hid], w=[junk, hid])
                    K.op("dve", lambda e: e.tensor_tensor(out=h_x2[:], in0=hid[:], in1=hid[:], op=ALU.mult), r=[hid], w=[h_x2])
                    K.op("dve", lambda e: e.tensor_scalar(out=h_x2[:], in0=h_x2[:], scalar1=0.044715, scalar2=1.0, op0=ALU.mult, op1=ALU.add),
                         r=[h_x2], w=[h_x2])
                    K.op("dve", lambda e: e.tensor_tensor(out=h_x2[:], in0=h_x2[:], in1=hid[:], op=ALU.mult), r=[h_x2, hid], w=[h_x2])
                    K.op("act", lambda e: e.activation(out=h_sg[:], in_=h_x2[:], func=AF.Sigmoid, scale=1.5957691216057308), r=[h_x2], w=[h_sg])
                    K.op("dve", lambda e: e.tensor_tensor(out=h_sg[:], in0=h_sg[:], in1=hid[:], op=ALU.mult), r=[h_sg, hid], w=[h_sg])
                    K.op("dve", lambda e: e.tensor_tensor(out=wts[:], in0=h_sg[:], in1=gate[:].rearrange("p h k -> p (h k)"), op=ALU.mult),
                         r=[h_sg, gate], w=[wts])
                    pF = [K.ps(), K.ps()]
                    for sl in range(128):
                        vb = Vg[(128 + sl) % NB]; vs = Vs[sl % 2]
                        K.dma("pool", None, None, r=[idx], w=[vb], fn=lambda e: e.indirect_dma_start(
                            out=vb[:], out_offset=None, in_=tab_v, in_offset=bass.IndirectOffsetOnAxis(ap=idx[:, sl:sl + 1], axis=0)))
                        K.op("act", lambda e: e.activation(out=vs[:], in_=vb[:], func=AF.Copy, scale=wts[:, sl:sl + 1]), r=[vb, wts], w=[vs])
                        for hh in range(2):
                            K.op("pe", lambda e: e.matmul(out=pF[hh][:], lhsT=identb[:], rhs=vs[:, hh * 512:(hh + 1) * 512],
                                                          start=(sl == 0), stop=(sl == 127)), r=[identb, vs], w=[pF[hh]], inc=(hh == 1))
                    for hh in range(2):
                        K.op("dve", lambda e: e.tensor_tensor(out=h2[:, hh * 512:(hh + 1) * 512], in0=pF[hh][:], in1=h1[:, hh * 512:(hh + 1) * 512],
                                                              op=ALU.add), r=[pF[hh], h1, h2], w=[h2])
                    rms_rstd(h2, ss, rstd, junk)
                    K.op("act", lambda e: e.activation(out=xn3[:], in_=h2[:], func=AF.Copy, scale=rstd[:, 0:1]), r=[h2, rstd], w=[xn3])
                    pX3 = K.ps(); pX3b = pX3[:].bitcast(BF16)
                    for kc in range(8):
                        K.op("pe", lambda e: e.transpose(out=pX3b[:, kc * 128:(kc + 1) * 128], in_=xn3[:, kc * 128:(kc + 1) * 128],
                                                         identity=identb[:]), r=[xn3, identb], w=[pX3], inc=(kc == 7))
                    K.op("act", lambda e: e.activation(out=xn3T[:], in_=pX3b[:, 0:1024].rearrange("p (c t) -> p c t", c=8), func=AF.Copy),
                         r=[pX3], w=[xn3T])
                    K.op("act", lambda e: e.activation(out=p_b[:], in_=p_t[:], func=AF.Copy), r=[p_t], w=[p_b])
                    pP = K.ps(); pPb = pP[:].bitcast(BF16)
                    for kc in range(2):
                        K.op("pe", lambda e: e.transpose(out=pPb[:, kc * 128:(kc + 1) * 128], in_=p_b[:, kc * 128:(kc + 1) * 128],
                                                         identity=identb[:]), r=[p_b, identb], w=[pP], inc=(kc == 1))
                    K.op("act", lambda e: e.activation(out=pT[:], in_=pPb[:, 0:256].rearrange("p (c t) -> p c t", c=2), func=AF.Copy),
                         r=[pP], w=[pT])
                    for hh in range(2):
                        pGt = K.ps()
                        for kc in range(8):
                            K.op("pe", lambda e: e.matmul(out=pGt[:], lhsT=xn3T[:, kc, :], rhs=pg[:, kc, hh * 512:(hh + 1) * 512],
                                                          start=(kc == 0), stop=(kc == 7)), r=[xn3T, pg], w=[pGt], inc=(kc == 7))
                        K.op("act", lambda e: e.activation(out=sgp[:, hh * 512:(hh + 1) * 512], in_=pGt[:], func=AF.Sigmoid), r=[pGt, sgp], w=[sgp])
                        pPP = K.ps()
                        for kc in range(2):
                            K.op("pe", lambda e: e.matmul(out=pPP[:], lhsT=pT[:, kc, :], rhs=pp[:, kc, hh * 512:(hh + 1) * 512],
                                                          start=(kc == 0), stop=(kc == 1)), r=[pT, pp], w=[pPP], inc=(kc == 1))
                        K.op("dve", lambda e: e.tensor_tensor(out=sgp[:, hh * 512:(hh + 1) * 512], in0=sgp[:, hh * 512:(hh + 1) * 512], in1=pPP[:],
                                                              op=ALU.mult), r=[sgp, pPP], w=[sgp])
                    K.op("dve", lambda e: e.tensor_tensor(out=h2[:], in0=h2[:], in1=sgp[:], op=ALU.add), r=[h2, sgp], w=[h2])
                    rms_rstd(h2, ss, rstd, junk)
                    K.op("dve", lambda e: e.scalar_tensor_tensor(out=o_t[:], in0=h2[:], scalar=rstd[:, 0:1], in1=gfin_b[:],
                                                                 op0=ALU.mult, op1=ALU.mult), r=[h2, rstd, gfin_b, o_t], w=[o_t])
                    K.dma("sp", out[t0:t0 + 128, :], o_t[:], r=[o_t])
            K.barrier()
        K.barrier()
    return nc


def host_inputs(inputs, b, T):
    f = lambda a: np.ascontiguousarray(np.asarray(a, dtype=np.float32))
    sk = np.asarray(inputs["peer_subkeys"], dtype=np.float32)[0]
    skT = np.ascontiguousarray(sk.transpose(0, 1, 3, 2).reshape(16, 128, 128))
    masks = np.zeros((128, 2), np.float32); masks[:64, 0] = 1.0; masks[64:, 1] = 1.0
    iota16 = np.tile(np.arange(16, dtype=np.float32), 16)
    return {
        "x": f(inputs["x"][b, :T]), "p": f(inputs["p"][0, b, :T]),
        "w_in": f(inputs["w_in"][0]), "conv_w": f(inputs["conv_w"][0]), "conv_b": f(inputs["conv_b"][0]),
        "shift_mu": f(inputs["shift_mu"][0]), "w0": f(inputs["w0"][0]), "w_up": f(inputs["w_up"][0]),
        "a0": f(inputs["a0"][0]), "a_up": f(inputs["a_up"][0]), "g_up": f(inputs["g_up"][0]),
        "k_k": f(inputs["k_k"][0]), "k_a": f(inputs["k_a"][0]), "r_k": f(inputs["r_k"][0].reshape(512)),
        "ln_g": f(inputs["ln_x_g"][0]), "ln_b": f(inputs["ln_x_b"][0]),
        "w_a": f(inputs["w_branch_a"][0]), "w_b": f(inputs["w_branch_b"][0]), "w_out": f(inputs["w_out"][0]),
        "gffn": f(inputs["norm_ffn_g"][0]), "wq": f(inputs["peer_wq"][0]), "skT": skT,
        "tab_u": f(inputs["peer_u"][0]), "tab_v": f(inputs["peer_v"][0]),
        "gple": f(inputs["norm_ple_g"][0]), "pgw": f(inputs["ple_gate_w"][0]), "ppw": f(inputs["ple_proj_w"][0]),
        "gfin": f(inputs["final_norm_g"]), "gmix": f(inputs["norm_mix_g"][0]),
        "ident": np.eye(128, dtype=np.float32), "masks": masks, "iota16": iota16,
    }


def kernel(**inputs):
    B, T = inputs["x"].shape[0], inputs["x"].shape[1]
    nc = build(T)
    in_maps = [host_inputs(inputs, b, T) for b in range(B)]
    res = run_bass_kernel_spmd(nc, in_maps, core_ids=list(range(B)))
    return np.stack([np.asarray(r["out"], dtype=np.float32) for r in res.results], axis=0)
```

```python
import numpy as np
from contextlib import ExitStack
import concourse.bass as bass
import concourse.mybir as mybir
from concourse.bass_utils import run_bass_kernel_spmd

F32 = mybir.dt.float32
BF16 = mybir.dt.bfloat16
U32 = mybir.dt.uint32
I32 = mybir.dt.int32
ALU = mybir.AluOpType
AF = mybir.ActivationFunctionType
AX = mybir.AxisListType

D = 1024
NEXP = 16384
SEM_EPOCH = 30000
ATTACH_WAIT = True


class Reg:
    __slots__ = ("w", "r")

    def __init__(self):
        self.w = None
        self.r = {}


class Tile:
    def __init__(self, t):
        self.t = t
        self.reg = Reg()

    def sub(self):
        return Tile(self.t)

    def __getitem__(self, k):
        return self.t[k]


class Ctx:
    def __init__(self, nc):
        self.nc = nc
        self.E = {"pe": nc.tensor, "act": nc.scalar, "dve": nc.vector, "pool": nc.gpsimd, "sp": nc.sync}
        self.nsem = 0
        self.sem = {}
        self.semkey = {}
        self.cnt = {}
        self.known = {e: {} for e in self.E}
        self.pend_r = {e: [] for e in self.E}
        self.pend_w = {e: [] for e in self.E}
        for e in self.E:
            self._newsem(e)
        self.dq = {}
        self.alldma = {}
        self.psb = []
        self.psi = 0
        self.defer = None
        self.nrot = 8

    def _alloc(self, name):
        h = self.nc.alloc_semaphore(name=f"{name}_{self.nsem}")
        self.nsem += 1
        return (f"{name}_{self.nsem}", h)

    def _newsem(self, e):
        key, h = self._alloc("s" + e)
        self.sem[e] = h
        self.semkey[e] = key
        self.cnt[e] = 0

    def _wait(self, e, ev):
        key, h, v = ev
        if self.known[e].get(key, 0) >= v:
            return
        self.known[e][key] = v
        if self.defer is not None:
            self.defer.append((h, v))
        else:
            self.E[e].wait_ge(h, v)

    def _flush_waits(self, e, keep_last):
        d = self.defer
        self.defer = None
        last = None
        if keep_last and d:
            last = d.pop()
        for (h, v) in d:
            self.E[e].wait_ge(h, v)
        return last

    def _need(self, e, ev, kind, is_dma):
        if (not is_dma) and ev[0] == self.semkey[e]:
            if e == "pe":
                return
        self._wait(e, ev)

    def _deps(self, e, reads, writes, is_dma=False):
        for r in reads:
            if r.reg.w is not None:
                self._need(e, r.reg.w, "raw", is_dma)
        for w in writes:
            if w.reg.w is not None:
                self._need(e, w.reg.w, "waw", is_dma)
            for ev in list(w.reg.r.values()):
                self._need(e, ev, "war", is_dma)

    def _commit(self, ev, reads, writes):
        for w in writes:
            w.reg.w = ev
            w.reg.r = {}
        for r in reads:
            r.reg.r[ev[0]] = ev

    def op(self, e, fn, r=(), w=(), inc=True):
        self.defer = []
        self._deps(e, r, w)
        last = self._flush_waits(e, ATTACH_WAIT)
        ins = fn(self.E[e])
        if last is not None:
            ins._wait_ge(last[0], last[1])
        if not inc:
            self.pend_r[e].extend(r)
            self.pend_w[e].extend(w)
            return None
        if self.cnt[e] >= SEM_EPOCH:
            self._newsem(e)
        self.cnt[e] += 1
        ins.then_inc(self.sem[e], 1)
        ev = (self.semkey[e], self.sem[e], self.cnt[e])
        self._commit(ev, list(r) + self.pend_r[e], list(w) + self.pend_w[e])
        self.pend_r[e] = []
        self.pend_w[e] = []
        return ev

    def dma(self, q, out, in_, r=(), w=(), fn=None, nslots=12):
        self._deps(q, r, w, is_dma=True)
        pool = self.dq.setdefault(q, {"slots": [None] * nslots, "i": 0})
        i = pool["i"] % nslots
        pool["i"] += 1
        slot = pool["slots"][i]
        if slot is None:
            key, h = self._alloc("d" + q)
            slot = [key, h, 0]
            pool["slots"][i] = slot
        if slot[2] > 0:
            self._wait(q, (slot[0], slot[1], 16 * slot[2]))
        if 16 * (slot[2] + 1) > SEM_EPOCH:
            key, h = self._alloc("d" + q)
            slot[0], slot[1], slot[2] = key, h, 0
        if fn is None:
            ins = self.E[q].dma_start(out=out, in_=in_)
        else:
            ins = fn(self.E[q])
        ins.then_inc(slot[1], 16)
        slot[2] += 1
        ev = (slot[0], slot[1], 16 * slot[2])
        self.alldma[slot[0]] = ev
        self._commit(ev, r, w)
        return ev

    def barrier(self):
        evs = [(self.semkey[e], self.sem[e], self.cnt[e]) for e in self.E if self.cnt[e] > 0]
        evs += list(self.alldma.values())
        for e in self.E:
            for ev in evs:
                if ev[0] == self.semkey[e] and e == "pe":
                    continue
                self._wait(e, ev)

    def init_psum(self, st):
        for i in range(8):
            self.psb.append(Tile(st.enter_context(self.nc.psum_tensor(f"psb{i}", [128, 512], F32))))

    def ps(self):
        t = self.psb[self.psi % self.nrot]
        self.psi += 1
        return t


def build(T, debug=False, phases="AB12"):
    NT = T // 128
    nc = bass.Bass("TRN2", target_bir_lowering=False)
    K = Ctx(nc)

    def din(name, shape, dt=F32):
        return nc.dram_tensor(name, list(shape), dt, kind="ExternalInput").ap()

    skind = "ExternalOutput" if debug else "Internal"

    def dscr(name, shape, dt=F32):
        return nc.dram_tensor(name, list(shape), dt, kind=skind).ap()

    x = din("x", [T, D]); p_in = din("p", [T, 256])
    w_in = din("w_in", [D, 5376]); conv_w = din("conv_w", [3, 512]); conv_b = din("conv_b", [512])
    shift_mu = din("shift_mu", [1792]); w0 = din("w0", [512]); w_up = din("w_up", [64, 512])
    a0 = din("a0", [512]); a_up = din("a_up", [64, 512]); g_up = din("g_up", [128, 512])
    k_k = din("k_k", [512]); k_a = din("k_a", [512]); r_k = din("r_k", [512])
    ln_g = din("ln_g", [512]); ln_b = din("ln_b", [512])
    w_a = din("w_a", [512, D]); w_b = din("w_b", [512, D]); w_out = din("w_out", [D, D])
    gffn = din("gffn", [D]); wq = din("wq", [D, 2048]); skT = din("skT", [16, 128, 128])
    tab_u = din("tab_u", [NEXP, D]); tab_v = din("tab_v", [NEXP, D])
    gple = din("gple", [D]); pgw = din("pgw", [D, D]); ppw = din("ppw", [256, D])
    gfin = din("gfin", [D]); gmix = din("gmix", [D])
    ident_d = din("ident", [128, 128]); masks_d = din("masks", [128, 2]); iota_d = din("iota16", [256])
    out = nc.dram_tensor("out", [T, D], F32, kind="ExternalOutput").ap()

    s_kr = dscr("s_kr", [512, T + 1, 4]); s_w = dscr("s_w", [512, T])
    s_bk = dscr("s_bk", [4, 6, T, 128]); s_v = dscr("s_v", [T, 512]); s_g = dscr("s_g", [T, 512])
    s_rk = dscr("s_rk", [T, 8]); s_pa = dscr("s_pa", [D, T], BF16); s_sgb = dscr("s_sgb", [D, T], BF16)
    s_y = dscr("s_y", [T, 512]); s_h1 = dscr("s_h1", [T, D])

    with ExitStack() as top:
        K.init_psum(top)
        cst = top
        ident = Tile(cst.enter_context(nc.sbuf_tensor("ident_sb", [128, 128], F32)))
        identb = Tile(cst.enter_context(nc.sbuf_tensor("identb", [128, 128], BF16)))
        masks = Tile(cst.enter_context(nc.sbuf_tensor("masks_sb", [128, 2], F32)))
        K.dma("sp", ident[:], ident_d, w=[ident])
        K.dma("sp", masks[:], masks_d, w=[masks])
        K.op("dve", lambda e: e.tensor_copy(out=identb[:], in_=ident[:]), r=[ident], w=[identb])

        def sbt(st, name, shape, dt):
            return Tile(st.enter_context(nc.sbuf_tensor(name, list(shape), dt)))

        def rms_rstd(src, ss, rstd, junk, eps=1e-6, n=1024.0):
            K.op("act", lambda e: e.activation(out=junk[:], in_=src[:], func=AF.Square, accum_out=ss[:]),
                 r=[src], w=[junk, ss])
            K.op("dve", lambda e: e.tensor_scalar(out=rstd[:], in0=ss[:], scalar1=1.0 / n, scalar2=eps,
                                                  op0=ALU.mult, op1=ALU.add), r=[ss], w=[rstd])
            K.op("act", lambda e: e.activation(out=rstd[:], in_=rstd[:], func=AF.Sqrt), r=[rstd], w=[rstd])
            K.op("dve", lambda e: e.reciprocal(out=rstd[:], in_=rstd[:]), r=[rstd], w=[rstd])

        def load_cast(st, dst, src_ap, nk, ncol, scale_tile=None, q="sp"):
            stg = [sbt(st, f"stg_{id(dst)}_{i}", [128, ncol], F32) for i in range(2)]
            for kc in range(nk):
                s_ = stg[kc % 2]
                K.dma(q, s_[:], src_ap[kc * 128:(kc + 1) * 128, :], w=[s_])
                if scale_tile is None:
                    K.op("act", lambda e: e.activation(out=dst[:, kc, :], in_=s_[:], func=AF.Copy), r=[s_], w=[dst])
                else:
                    K.op("act", lambda e: e.activation(out=dst[:, kc, :], in_=s_[:], func=AF.Copy,
                                                       scale=scale_tile[:, kc:kc + 1]), r=[s_, scale_tile], w=[dst])

        def bcast(st, name, src_ap, n):
            t = sbt(st, name, [128, n], F32)
            K.dma("sp", t[:], src_ap.partition_broadcast(128), w=[t])
            return t

        def colvec(st, name, src_ap, nk):
            t = sbt(st, name, [128, nk], F32)
            K.dma("sp", t[:], src_ap.rearrange("(c p) -> p c", p=128), w=[t],
                  fn=lambda e: e.dma_start(out=t[:], in_=src_ap.rearrange("(c p) -> p c", p=128),
                                           allow_slow_non_contiguous=True))
            return t

        if "A" in phases:
            with ExitStack() as sa:
                Wc = sbt(sa, "Wc", [128, 8, 1536], BF16)
                W1 = sbt(sa, "W1", [128, 8, 1792], BF16)
                W2 = sbt(sa, "W2", [128, 8, 1792], BF16)
                Wg = sbt(sa, "Wg", [128, 8, 2048], BF16)
                wa = sbt(sa, "wa", [128, 4, 1024], BF16)
                waup = sbt(sa, "waup", [128, 512], BF16)
                gup = sbt(sa, "gup", [128, 512], BF16)
                gm = colvec(sa, "gm", gmix, 8)
                cw = sbt(sa, "cw", [128, 3, 4], F32)
                K.dma("sp", None, None, w=[cw], fn=lambda e: e.dma_start(
                    out=cw[:], in_=conv_w.rearrange("j (c p) -> p j c", p=128), allow_slow_non_contiguous=True))
                cb = colvec(sa, "cb", conv_b, 4)
                w0_b = bcast(sa, "w0_b", w0, 512); a0_b = bcast(sa, "a0_b", a0, 512)
                kk_b = bcast(sa, "kk_b", k_k, 512); ka_b = bcast(sa, "ka_b", k_a, 512)
                rk_b = bcast(sa, "rk_b", r_k, 512)
                with ExitStack() as sp_:
                    mu_b = bcast(sp_, "mu_b", shift_mu, 1792)
                    stg = [sbt(sp_, f"stgw{i}", [128, 5376], F32) for i in range(2)]
                    t2 = sbt(sp_, "t2w", [128, 1792], F32)
                    for kc in range(8):
                        s_ = stg[kc % 2]
                        K.dma("sp", s_[:], w_in[kc * 128:(kc + 1) * 128, :], w=[s_])
                        K.op("act", lambda e: e.activation(out=Wc[:, kc, :], in_=s_[:, 0:1536], func=AF.Copy,
                                                           scale=gm[:, kc:kc + 1]), r=[s_, gm], w=[Wc])
                        K.op("act", lambda e: e.activation(out=Wg[:, kc, :], in_=s_[:, 3328:5376], func=AF.Copy,
                                                           scale=gm[:, kc:kc + 1]), r=[s_, gm], w=[Wg])
                        K.op("dve", lambda e: e.scalar_tensor_tensor(out=t2[:], in0=s_[:, 1536:3328], scalar=gm[:, kc:kc + 1],
                                                                     in1=mu_b[:], op0=ALU.mult, op1=ALU.mult),
                             r=[s_, gm, mu_b], w=[t2])
                        K.op("pool", lambda e: e.tensor_copy(out=W2[:, kc, :], in_=t2[:]), r=[t2], w=[W2])
                        K.op("dve", lambda e: e.scalar_tensor_tensor(out=W1[:, kc, :], in0=s_[:, 1536:3328], scalar=gm[:, kc:kc + 1],
                                                                     in1=t2[:], op0=ALU.mult, op1=ALU.subtract),
                             r=[s_, gm, t2], w=[W1])
                    load_cast(sp_, wa, w_a, 4, 1024)
                    stq = sbt(sp_, "stq", [128, 512], F32)
                    K.dma("sp", stq[0:64, :], w_up, w=[stq])
                    K.dma("sp", stq[64:128, :], a_up, w=[stq])
                    K.op("act", lambda e: e.activation(out=waup[:], in_=stq[:], func=AF.Copy), r=[stq], w=[waup])
                    stq2 = sbt(sp_, "stq2", [128, 512], F32)
                    K.dma("sp", stq2[:], g_up, w=[stq2])
                    K.op("act", lambda e: e.activation(out=gup[:], in_=stq2[:], func=AF.Copy), r=[stq2], w=[gup])
                K.barrier()

                xt = [sbt(sa, f"xt{i}", [128, 1024], F32) for i in range(2)]
                junk = sbt(sa, "junkA", [128, 1024], BF16)
                ss = sbt(sa, "ssA", [128, 1], F32); rstd = sbt(sa, "rstdA", [128, 1], F32)
                xn = sbt(sa, "xnA", [128, 1024], BF16)
                xnT = [sbt(sa, f"xnT{i}", [128, 8, 129], BF16) for i in range(2)]
                gbs = sbt(sa, "gbs", [128, 4, 128], F32); gcs = sbt(sa, "gcs", [128, 4, 128], F32)
                u = sbt(sa, "uconv", [128, 4, 130], F32)
                yt = sbt(sa, "yconv_t", [128, 4, 128], F32)
                yc = sbt(sa, "yconv", [128, 4, 128], BF16)
                sga = sbt(sa, "sga", [128, 4, 128], F32)
                pa_st = sbt(sa, "pa_st", [128, 8, 128], BF16); sgb_st = sbt(sa, "sgb_st", [128, 8, 128], BF16)
                waT = sbt(sa, "waT", [128, 128], BF16); sgd = sbt(sa, "sgd", [128, 128], BF16)
                t_w = sbt(sa, "t_w", [128, 512], F32); t_a = sbt(sa, "t_a", [128, 512], F32)
                t_kk = sbt(sa, "t_kk", [128, 512], F32); t_sq = sbt(sa, "t_sq", [128, 512], F32)
                t_r = sbt(sa, "t_r", [128, 512], F32); t_v = sbt(sa, "t_v", [128, 512], F32)
                t_g = sbt(sa, "t_g", [128, 512], F32); t_km = sbt(sa, "t_km", [128, 512], F32)
                t_3 = sbt(sa, "t_3", [128, 512], F32); t_4 = sbt(sa, "t_4", [128, 512], F32)
                s8 = sbt(sa, "s8", [128, 8], F32); rn8 = sbt(sa, "rn8", [128, 8], F32); rk8 = sbt(sa, "rk8", [128, 8], F32)
                bkz = sbt(sa, "bkz", [128, 4, 6, 128], F32)
                kr_st = sbt(sa, "kr_st", [128, 4, 129, 4], F32)
                w_st = sbt(sa, "w_st", [128, 4, 128], F32)
                K.op("pool", lambda e: e.memset(bkz[:], 0.0), w=[bkz])
                K.op("pool", lambda e: e.memset(kr_st[:], 0.0), w=[kr_st])
                K.op("pool", lambda e: e.memset(u[:], 0.0), w=[u])
                K.op("pool", lambda e: e.memset(xnT[1][:], 0.0), w=[xnT[1]])

                K.dma("sp", xt[0][:], x[0:128, :], w=[xt[0]])
                for i in range(NT):
                    t0 = i * 128
                    xb = xt[i % 2]; xT = xnT[i % 2]; xTp = xnT[(i + 1) % 2]
                    if i + 1 < NT:
                        K.dma("sp", xt[(i + 1) % 2][:], x[t0 + 128:t0 + 256, :], w=[xt[(i + 1) % 2]])
                    rms_rstd(xb, ss, rstd, junk)
                    K.op("act", lambda e: e.activation(out=xn[:], in_=xb[:], func=AF.Copy, scale=rstd[:, 0:1]),
                         r=[xb, rstd], w=[xn])
                    tp = K.ps()
                    tpb = tp[:].bitcast(BF16)
                    for kc in range(8):
                        K.op("pe", lambda e: e.transpose(out=tpb[:, kc * 128:(kc + 1) * 128], in_=xn[:, kc * 128:(kc + 1) * 128],
                                                         identity=identb[:]), r=[xn, identb], w=[tp], inc=(kc == 7))
                    K.op("dve", lambda e: e.tensor_copy(out=xT[:, :, 0:1], in_=xTp[:, :, 128:129]), r=[xTp], w=[xT])
                    K.op("dve", lambda e: e.tensor_copy(out=xT[:, :, 1:129],
                                                        in_=tpb[:, 0:1024].rearrange("p (c t) -> p c t", c=8)),
                         r=[tp], w=[xT])

                    def fm_group(W, col0, nch, ps):
                        for j in range(nch):
                            for kc in range(8):
                                K.op("pe", lambda e: e.matmul(out=ps[:, j * 128:(j + 1) * 128],
                                                              lhsT=W[:, kc, col0 + j * 128: col0 + (j + 1) * 128],
                                                              rhs=xT[:, kc, 1:129], start=(kc == 0), stop=(kc == 7)),
                                     r=[W, xT], w=[ps], inc=(kc == 7 and j == nch - 1))

                    pA = K.ps(); fm_group(Wc, 0, 4, pA)
                    K.op("act", lambda e: e.activation(out=gbs[:], in_=pA[:].rearrange("p (c t) -> p c t", c=4), func=AF.Copy),
                         r=[pA], w=[gbs])
                    pB = K.ps(); fm_group(Wc, 512, 4, pB)
                    K.op("act", lambda e: e.activation(out=gcs[:], in_=pB[:].rearrange("p (c t) -> p c t", c=4), func=AF.Copy),
                         r=[pB], w=[gcs])
                    pC = K.ps(); fm_group(Wc, 1024, 4, pC)
                    K.op("dve", lambda e: e.tensor_copy(out=u[:, :, 0:2], in_=u[:, :, 128:130]), r=[u], w=[u])
                    K.op("dve", lambda e: e.tensor_tensor(out=u[:, :, 2:130], in0=gcs[:],
                                                          in1=pC[:].rearrange("p (c t) -> p c t", c=4), op=ALU.mult),
                         r=[gcs, pC, u], w=[u])
                    for cc in range(4):
                        K.op("dve", lambda e: e.tensor_scalar(out=yt[:, cc, :], in0=u[:, cc, 2:130], scalar1=cw[:, 2, cc:cc + 1],
                                                              scalar2=None, op0=ALU.mult), r=[u, cw], w=[yt])
                        K.op("dve", lambda e: e.scalar_tensor_tensor(out=yt[:, cc, :], in0=u[:, cc, 1:129], scalar=cw[:, 1, cc:cc + 1],
                                                                     in1=yt[:, cc, :], op0=ALU.mult, op1=ALU.add),
                             r=[u, cw, yt], w=[yt])
                        K.op("dve", lambda e: e.scalar_tensor_tensor(out=yt[:, cc, :], in0=u[:, cc, 0:128], scalar=cw[:, 0, cc:cc + 1],
                                                                     in1=yt[:, cc, :], op0=ALU.mult, op1=ALU.add),
                             r=[u, cw, yt], w=[yt])
                        K.op("dve", lambda e: e.scalar_tensor_tensor(out=yc[:, cc, :], in0=yt[:, cc, :], scalar=cb[:, cc:cc + 1],
                                                                     in1=gbs[:, cc, :], op0=ALU.add, op1=ALU.mult),
                             r=[yt, cb, gbs], w=[yc])
                    for hh in range(2):
                        pY = K.ps()
                        for j in range(4):
                            fc = hh * 4 + j
                            for kc in range(4):
                                K.op("pe", lambda e: e.matmul(out=pY[:, j * 128:(j + 1) * 128], lhsT=wa[:, kc, fc * 128:(fc + 1) * 128],
                                                              rhs=yc[:, kc, :], start=(kc == 0), stop=(kc == 3)),
                                     r=[wa, yc], w=[pY], inc=(kc == 3 and j == 3))
                        pG = K.ps(); fm_group(Wg, hh * 512, 4, pG)
                        K.op("act", lambda e: e.activation(out=sga[:], in_=pG[:].rearrange("p (c t) -> p c t", c=4), func=AF.Sigmoid),
                             r=[pG], w=[sga])
                        K.op("dve", lambda e: e.tensor_tensor(out=pa_st[:, hh * 4:(hh + 1) * 4, :], in0=sga[:],
                                                              in1=pY[:].rearrange("p (c t) -> p c t", c=4), op=ALU.mult),
                             r=[sga, pY, pa_st], w=[pa_st])
                        pG2 = K.ps(); fm_group(Wg, 1024 + hh * 512, 4, pG2)
                        K.op("act", lambda e: e.activation(out=sgb_st[:, hh * 4:(hh + 1) * 4, :],
                                                           in_=pG2[:].rearrange("p (c t) -> p c t", c=4), func=AF.Sigmoid),
                             r=[pG2, sgb_st], w=[sgb_st])
                    K.dma("sp", s_pa.rearrange("(c p) t -> p c t", p=128)[:, :, t0:t0 + 128], pa_st[:], r=[pa_st])
                    K.dma("sp", s_sgb.rearrange("(c p) t -> p c t", p=128)[:, :, t0:t0 + 128], sgb_st[:], r=[sgb_st])

                    def tm_group(col0, ps):
                        for kc in range(8):
                            K.op("pe", lambda e: e.matmul(out=ps[:], lhsT=xT[:, kc, 1:129], rhs=W1[:, kc, col0:col0 + 512],
                                                          start=(kc == 0), stop=False), r=[W1, xT], w=[ps], inc=False)
                        for kc in range(8):
                            K.op("pe", lambda e: e.matmul(out=ps[:], lhsT=xT[:, kc, 0:128], rhs=W2[:, kc, col0:col0 + 512],
                                                          start=False, stop=(kc == 7)), r=[W2, xT], w=[ps], inc=(kc == 7))
                    pS = K.ps()
                    for j in range(2):
                        c0 = 1536 + j * 128
                        for kc in range(8):
                            K.op("pe", lambda e: e.matmul(out=pS[:, j * 128:(j + 1) * 128], lhsT=W1[:, kc, c0:c0 + 128],
                                                          rhs=xT[:, kc, 1:129], start=(kc == 0), stop=False),
                                 r=[W1, xT], w=[pS], inc=False)
                        for kc in range(8):
                            K.op("pe", lambda e: e.matmul(out=pS[:, j * 128:(j + 1) * 128], lhsT=W2[:, kc, c0:c0 + 128],
                                                          rhs=xT[:, kc, 0:128], start=False, stop=(kc == 7)),
                                 r=[W2, xT], w=[pS], inc=(kc == 7 and j == 1))
                    K.op("act", lambda e: e.activation(out=waT[0:64, :], in_=pS[0:64, 0:128], func=AF.Tanh), r=[pS], w=[waT])
                    K.op("act", lambda e: e.activation(out=waT[64:128, :], in_=pS[64:128, 0:128], func=AF.Copy), r=[pS, waT], w=[waT])
                    K.op("act", lambda e: e.activation(out=sgd[:], in_=pS[:, 128:256], func=AF.Sigmoid), r=[pS], w=[sgd])
                    pD = K.ps()
                    K.op("pe", lambda e: e.matmul(out=pD[:], lhsT=waT[0:64, :], rhs=waup[0:64, :], start=True, stop=True),
                         r=[waT, waup], w=[pD])
                    K.op("dve", lambda e: e.tensor_tensor(out=t_w[:], in0=pD[:], in1=w0_b[:], op=ALU.add), r=[pD, w0_b], w=[t_w])
                    K.op("act", lambda e: e.activation(out=t_w[:], in_=t_w[:], func=AF.Sigmoid), r=[t_w], w=[t_w])
                    K.op("act", lambda e: e.activation(out=t_w[:], in_=t_w[:], func=AF.Exp, scale=-0.6065306597126334),
                         r=[t_w], w=[t_w])
                    pAa = K.ps()
                    K.op("pe", lambda e: e.matmul(out=pAa[:], lhsT=waT[64:128, :], rhs=waup[64:128, :], start=True, stop=True),
                         r=[waT, waup], w=[pAa])
                    K.op("dve", lambda e: e.tensor_tensor(out=t_a[:], in0=pAa[:], in1=a0_b[:], op=ALU.add), r=[pAa, a0_b], w=[t_a])
                    K.op("act", lambda e: e.activation(out=t_a[:], in_=t_a[:], func=AF.Sigmoid), r=[t_a], w=[t_a])
                    pGg = K.ps()
                    K.op("pe", lambda e: e.matmul(out=pGg[:], lhsT=sgd[:], rhs=gup[:], start=True, stop=True), r=[sgd, gup], w=[pGg])
                    K.op("act", lambda e: e.activation(out=t_g[:], in_=pGg[:], func=AF.Copy), r=[pGg], w=[t_g])
                    K.dma("sp", s_g[t0:t0 + 128, :], t_g[:], r=[t_g])
                    pR = K.ps(); tm_group(0, pR)
                    K.op("act", lambda e: e.activation(out=t_r[:], in_=pR[:], func=AF.Copy), r=[pR], w=[t_r])
                    pK = K.ps(); tm_group(512, pK)
                    K.op("dve", lambda e: e.tensor_tensor(out=t_kk[:], in0=pK[:], in1=kk_b[:], op=ALU.mult), r=[pK, kk_b], w=[t_kk])
                    K.op("act", lambda e: e.activation(out=t_sq[:], in_=t_kk[:], func=AF.Square), r=[t_kk], w=[t_sq])
                    K.op("dve", lambda e: e.tensor_reduce(out=s8[:], in_=t_sq[:].rearrange("p (h n) -> p h n", h=8), axis=AX.X, op=ALU.add),
                         r=[t_sq], w=[s8])
                    K.op("dve", lambda e: e.tensor_scalar(out=rn8[:], in0=s8[:], scalar1=1e-12, scalar2=None, op0=ALU.add),
                         r=[s8], w=[rn8])
                    K.op("act", lambda e: e.activation(out=rn8[:], in_=rn8[:], func=AF.Sqrt), r=[rn8], w=[rn8])
                    K.op("dve", lambda e: e.reciprocal(out=rn8[:], in_=rn8[:]), r=[rn8], w=[rn8])
                    K.op("dve", lambda e: e.tensor_tensor(out=t_kk[:].rearrange("p (h n) -> p h n", h=8),
                                                          in0=t_kk[:].rearrange("p (h n) -> p h n", h=8),
                                                          in1=rn8[:].unsqueeze(2).to_broadcast([128, 8, 64]), op=ALU.mult),
                         r=[t_kk, rn8], w=[t_kk])
                    K.op("dve", lambda e: e.scalar_tensor_tensor(out=t_3[:], in0=t_a[:], scalar=-1.0, in1=ka_b[:], op0=ALU.add, op1=ALU.mult),
                         r=[t_a, ka_b], w=[t_3])
                    K.op("dve", lambda e: e.scalar_tensor_tensor(out=t_km[:], in0=t_3[:], scalar=1.0, in1=pK[:], op0=ALU.add, op1=ALU.mult),
                         r=[t_3, pK], w=[t_km])
                    K.op("pool", lambda e: e.tensor_tensor(out=t_4[:], in0=t_r[:], in1=t_km[:], op=ALU.mult), r=[t_r, t_km], w=[t_4])
                    K.op("pool", lambda e: e.tensor_tensor(out=t_4[:], in0=t_4[:], in1=rk_b[:], op=ALU.mult), r=[t_4, rk_b], w=[t_4])
                    K.op("dve", lambda e: e.tensor_reduce(out=rk8[:], in_=t_4[:].rearrange("p (h n) -> p h n", h=8), axis=AX.X, op=ALU.add),
                         r=[t_4], w=[rk8])
                    K.dma("sp", s_rk[t0:t0 + 128, :], rk8[:], r=[rk8])
                    kk5 = t_kk[:].rearrange("p (q j n) -> p q j n", q=4, j=2)
                    a5 = t_a[:].rearrange("p (q j n) -> p q j n", q=4, j=2)
                    km5 = t_km[:].rearrange("p (q j n) -> p q j n", q=4, j=2)
                    for j in range(2):
                        K.op("dve", lambda e: e.scalar_tensor_tensor(out=bkz[:, :, j, j * 64:(j + 1) * 64], in0=kk5[:, :, j, :], scalar=-1.0,
                                                                     in1=a5[:, :, j, :], op0=ALU.mult, op1=ALU.mult),
                             r=[t_kk, t_a, bkz], w=[bkz])
                        K.op("pool", lambda e: e.tensor_copy(out=bkz[:, :, 4 + j, j * 64:(j + 1) * 64], in_=km5[:, :, j, :]),
                             r=[t_km, bkz], w=[bkz])
                    K.dma("sp", s_bk.rearrange("q j t c -> t q j c")[t0:t0 + 128], bkz[:], r=[bkz])
                    pV = K.ps(); tm_group(1024, pV)
                    K.op("act", lambda e: e.activation(out=t_v[:], in_=pV[:], func=AF.Copy), r=[pV], w=[t_v])
                    K.dma("sp", s_v[t0:t0 + 128, :], t_v[:], r=[t_v])
                    K.op("pool", lambda e: e.tensor_copy(out=kr_st[:, :, 0, 2:4], in_=kr_st[:, :, 128, 2:4]), r=[kr_st], w=[kr_st])
                    for (src, kind) in ((t_kk, 0), (t_r, 1), (t_w, 2)):
                        pT = K.ps()
                        for c in range(4):
                            K.op("pe", lambda e: e.transpose(out=pT[:, c * 128:(c + 1) * 128], in_=src[:, c * 128:(c + 1) * 128],
                                                             identity=ident[:]), r=[src, ident], w=[pT], inc=(c == 3))
                        pT3 = pT[:].rearrange("p (c t) -> p c t", c=4)
                        if kind == 0:
                            for j in range(2):
                                K.op("act", lambda e: e.activation(out=kr_st[:, :, 0:128, j], in_=pT3, func=AF.Copy, scale=masks[:, j:j + 1]),
                                     r=[pT, masks, kr_st], w=[kr_st])
                        elif kind == 1:
                            for j in range(2):
                                K.op("act", lambda e: e.activation(out=kr_st[:, :, 1:129, 2 + j], in_=pT3, func=AF.Copy, scale=masks[:, j:j + 1]),
                                     r=[pT, masks, kr_st], w=[kr_st])
                        else:
                            K.op("act", lambda e: e.activation(out=w_st[:], in_=pT3, func=AF.Copy), r=[pT], w=[w_st])
                    K.dma("sp", s_kr.rearrange("(c p) t j -> p c t j", p=128)[:, :, t0:t0 + 128, :], kr_st[:, :, 0:128, :], r=[kr_st])
                    if i == NT - 1:
                        K.dma("sp", s_kr.rearrange("(c p) t j -> p c t j", p=128)[:, :, T:T + 1, :], kr_st[:, :, 128:129, :], r=[kr_st])
                    K.dma("sp", s_w.rearrange("(c p) t -> p c t", p=128)[:, :, t0:t0 + 128], w_st[:], r=[w_st])
            K.barrier()

        if "B" in phases:
            with ExitStack() as sb_:
                SUB = 32
                Tst = [sbt(sb_, f"Tst{q}", [128, 64], F32) for q in range(4)]
                KRt = [[sbt(sb_, f"KRt{q}_{b}", [128, 128, 4], F32) for b in range(2)] for q in range(4)]
                KRl = [sbt(sb_, f"KRl{q}", [128, 1, 4], F32) for q in range(4)]
                Yl = [sbt(sb_, f"Yl{q}", [4, 64], F32) for q in range(4)]
                Wt = [[sbt(sb_, f"Wt{q}_{b}", [128, 128], F32) for b in range(2)] for q in range(4)]
                BKh = [[sbt(sb_, f"BK{h}_{b}", [64, SUB, 128], F32) for b in range(2)] for h in range(2)]
                SVh = [[sbt(sb_, f"SV{h}_{b}", [64, SUB, 64], F32) for b in range(2)] for h in range(2)]
                BKr = [[BKh[q // 2][b].sub() for b in range(2)] for q in range(4)]
                SVr = [[SVh[q // 2][b].sub() for b in range(2)] for q in range(4)]
                psq = [K.ps() for _ in range(4)]
                skk_r = [psq[q].sub() for q in range(4)]; U_r = [psq[q].sub() for q in range(4)]
                for q in range(4):
                    K.op("dve", lambda e: e.memset(Tst[q][:], 0.0), w=[Tst[q]])
                NSUB = 128 // SUB
                for c in range(NT):
                    t0 = c * 128; par = c % 2
                    for q in range(4):
                        K.dma("sp", KRt[q][par][:], s_kr[q * 128:(q + 1) * 128, t0:t0 + 128, :], w=[KRt[q][par]])
                        K.dma("sp", Wt[q][par][:], s_w[q * 128:(q + 1) * 128, t0:t0 + 128], w=[Wt[q][par]])
                    for sc in range(NSUB):
                        sg = c * NSUB + sc; sp2 = sg % 2; ts0 = t0 + sc * SUB
                        for q in range(4):
                            base = 32 * (q % 2)
                            K.dma("sp", BKh[q // 2][sp2][base:base + 6, :, :], s_bk[q, :, ts0:ts0 + SUB, :], w=[BKr[q][sp2]])
                            K.dma("sp", SVh[q // 2][sp2][base + 4:base + 6, :, :],
                                  s_v[ts0:ts0 + SUB, q * 128:(q + 1) * 128].rearrange("t (j n) -> j t n", j=2), w=[SVr[q][sp2]])
                        for ts in range(SUB):
                            tl = sc * SUB + ts
                            for q in range(4):
                                K.op("pe", lambda e: e.matmul(out=psq[q][0:4, 0:64], lhsT=KRt[q][par][:, tl, :], rhs=Tst[q][:], start=True, stop=True),
                                     r=[KRt[q][par], Tst[q]], w=[skk_r[q]])
                            for q in range(4):
                                base = 32 * (q % 2); SV = SVh[q // 2][sp2]
                                K.op("act", lambda e: e.activation(out=SV[base:base + 4, ts, :], in_=psq[q][0:4, 0:64], func=AF.Copy),
                                     r=[skk_r[q]], w=[SVr[q][sp2]])
                            for q in range(4):
                                base = 32 * (q % 2); BK = BKh[q // 2][sp2]; SV = SVh[q // 2][sp2]
                                K.op("pe", lambda e: e.matmul(out=psq[q][:, 64:128], lhsT=BK[base:base + 6, ts, :], rhs=SV[base:base + 6, ts, :],
                                                              start=True, stop=True), r=[BKr[q][sp2], SVr[q][sp2]], w=[U_r[q]])
                            for q in range(4):
                                K.op("dve", lambda e: e.scalar_tensor_tensor(out=Tst[q][:], in0=Tst[q][:], scalar=Wt[q][par][:, tl:tl + 1],
                                                                             in1=psq[q][:, 64:128], op0=ALU.mult, op1=ALU.add),
                                     r=[U_r[q], Wt[q][par], Tst[q]], w=[Tst[q]])
                        for q in range(4):
                            base = 32 * (q % 2); SV = SVh[q // 2][sp2]
                            s0 = 1 if sg == 0 else 0
                            K.dma("sp", s_y[ts0 - 1 + s0:ts0 + SUB - 1, q * 128:(q + 1) * 128].rearrange("t (j n) -> j t n", j=2),
                                  SV[base + 2:base + 4, s0:SUB, :], r=[SVr[q][sp2]])
                for q in range(4):
                    K.dma("sp", KRl[q][:], s_kr[q * 128:(q + 1) * 128, T:T + 1, :], w=[KRl[q]])
                for q in range(4):
                    K.op("pe", lambda e: e.matmul(out=psq[q][0:4, 0:64], lhsT=KRl[q][:, 0, :], rhs=Tst[q][:], start=True, stop=True),
                         r=[KRl[q], Tst[q]], w=[skk_r[q]])
                    K.op("act", lambda e: e.activation(out=Yl[q][:], in_=psq[q][0:4, 0:64], func=AF.Copy), r=[skk_r[q]], w=[Yl[q]])
                    K.dma("sp", s_y[T - 1:T, q * 128:(q + 1) * 128].rearrange("t (j n) -> j t n", j=2), Yl[q][2:4, :].unsqueeze(1), r=[Yl[q]])
            K.barrier()

        if "1" in phases:
            with ExitStack() as sc_:
                wb = sbt(sc_, "wb", [128, 4, 1024], BF16); wo = sbt(sc_, "wo", [128, 8, 1024], BF16)
                lng_b = bcast(sc_, "lng_b", ln_g, 512); lnb_b = bcast(sc_, "lnb_b", ln_b, 512)
                with ExitStack() as sp_:
                    load_cast(sp_, wb, w_b, 4, 1024)
                    load_cast(sp_, wo, w_out, 8, 1024)
                K.barrier()
                y_t = sbt(sc_, "y_t", [128, 512], F32); v_t = sbt(sc_, "v_tC", [128, 512], F32)
                g_t = sbt(sc_, "g_tC", [128, 512], F32); rk_t = sbt(sc_, "rk_tC", [128, 8], F32)
                m8 = sbt(sc_, "m8", [128, 8], F32); v8 = sbt(sc_, "v8", [128, 8], F32)
                c_t = sbt(sc_, "c_t", [128, 512], F32); sq_t = sbt(sc_, "sq_t", [128, 512], F32)
                mix = sbt(sc_, "mixC", [128, 512], BF16); mixT = sbt(sc_, "mixT", [128, 4, 128], BF16)
                pa_t = sbt(sc_, "pa_tC", [128, 8, 128], BF16); sgb_t = sbt(sc_, "sgb_tC", [128, 8, 128], BF16)
                mg = sbt(sc_, "mgC", [128, 8, 128], F32); mgb = sbt(sc_, "mgbC", [128, 8, 128], BF16)
                x_t = sbt(sc_, "x_tC", [128, 1024], F32); h1 = sbt(sc_, "h1C", [128, 1024], F32)
                for i in range(NT):
                    t0 = i * 128
                    K.dma("sp", y_t[:], s_y[t0:t0 + 128, :], w=[y_t])
                    K.dma("sp", v_t[:], s_v[t0:t0 + 128, :], w=[v_t])
                    K.dma("sp", g_t[:], s_g[t0:t0 + 128, :], w=[g_t])
                    K.dma("sp", rk_t[:], s_rk[t0:t0 + 128, :], w=[rk_t])
                    K.dma("sp", pa_t[:], s_pa.rearrange("(c p) t -> p c t", p=128)[:, :, t0:t0 + 128], w=[pa_t])
                    K.dma("sp", sgb_t[:], s_sgb.rearrange("(c p) t -> p c t", p=128)[:, :, t0:t0 + 128], w=[sgb_t])
                    K.dma("sp", x_t[:], x[t0:t0 + 128, :], w=[x_t])
                    y3 = y_t[:].rearrange("p (h n) -> p h n", h=8)
                    c3 = c_t[:].rearrange("p (h n) -> p h n", h=8)
                    K.op("dve", lambda e: e.tensor_reduce(out=m8[:], in_=y3, axis=AX.X, op=ALU.add), r=[y_t], w=[m8])
                    K.op("dve", lambda e: e.tensor_scalar(out=m8[:], in0=m8[:], scalar1=1.0 / 64.0, scalar2=None, op0=ALU.mult), r=[m8], w=[m8])
                    K.op("dve", lambda e: e.tensor_tensor(out=c3, in0=y3, in1=m8[:].unsqueeze(2).to_broadcast([128, 8, 64]), op=ALU.subtract),
                         r=[y_t, m8], w=[c_t])
                    K.op("act", lambda e: e.activation(out=sq_t[:], in_=c_t[:], func=AF.Square), r=[c_t], w=[sq_t])
                    K.op("dve", lambda e: e.tensor_reduce(out=v8[:], in_=sq_t[:].rearrange("p (h n) -> p h n", h=8), axis=AX.X, op=ALU.add),
                         r=[sq_t], w=[v8])
                    K.op("dve", lambda e: e.tensor_scalar(out=v8[:], in0=v8[:], scalar1=1.0 / 64.0, scalar2=64e-5, op0=ALU.mult, op1=ALU.add),
                         r=[v8], w=[v8])
                    K.op("act", lambda e: e.activation(out=v8[:], in_=v8[:], func=AF.Sqrt), r=[v8], w=[v8])
                    K.op("dve", lambda e: e.reciprocal(out=v8[:], in_=v8[:]), r=[v8], w=[v8])
                    K.op("dve", lambda e: e.tensor_tensor(out=c3, in0=c3, in1=v8[:].unsqueeze(2).to_broadcast([128, 8, 64]), op=ALU.mult),
                         r=[c_t, v8], w=[c_t])
                    K.op("dve", lambda e: e.tensor_tensor(out=c_t[:], in0=c_t[:], in1=lng_b[:], op=ALU.mult), r=[c_t, lng_b], w=[c_t])
                    K.op("dve", lambda e: e.tensor_tensor(out=c_t[:], in0=c_t[:], in1=lnb_b[:], op=ALU.add), r=[c_t, lnb_b], w=[c_t])
                    K.op("pool", lambda e: e.tensor_tensor(out=sq_t[:].rearrange("p (h n) -> p h n", h=8),
                                                           in0=v_t[:].rearrange("p (h n) -> p h n", h=8),
                                                           in1=rk_t[:].unsqueeze(2).to_broadcast([128, 8, 64]), op=ALU.mult),
                         r=[v_t, rk_t, sq_t], w=[sq_t])
                    K.op("dve", lambda e: e.tensor_tensor(out=c_t[:], in0=c_t[:], in1=sq_t[:], op=ALU.add), r=[c_t, sq_t], w=[c_t])
                    K.op("dve", lambda e: e.tensor_tensor(out=mix[:], in0=c_t[:], in1=g_t[:], op=ALU.mult), r=[c_t, g_t], w=[mix])
                    pM = K.ps(); pMb = pM[:].bitcast(BF16)
                    for c in range(4):
                        K.op("pe", lambda e: e.transpose(out=pMb[:, c * 128:(c + 1) * 128], in_=mix[:, c * 128:(c + 1) * 128],
                                                         identity=identb[:]), r=[mix, identb], w=[pM], inc=(c == 3))
                    K.op("act", lambda e: e.activation(out=mixT[:], in_=pMb[:, 0:512].rearrange("p (c t) -> p c t", c=4), func=AF.Copy),
                         r=[pM], w=[mixT])
                    for hh in range(2):
                        pYb = K.ps()
                        for j in range(4):
                            fc = hh * 4 + j
                            for kc in range(4):
                                K.op("pe", lambda e: e.matmul(out=pYb[:, j * 128:(j + 1) * 128], lhsT=wb[:, kc, fc * 128:(fc + 1) * 128],
                                                              rhs=mixT[:, kc, :], start=(kc == 0), stop=(kc == 3)),
                                     r=[wb, mixT], w=[pYb], inc=(kc == 3 and j == 3))
                        K.op("dve", lambda e: e.tensor_tensor(out=mg[:, hh * 4:(hh + 1) * 4, :], in0=sgb_t[:, hh * 4:(hh + 1) * 4, :],
                                                              in1=pYb[:].rearrange("p (c t) -> p c t", c=4), op=ALU.mult),
                             r=[sgb_t, pYb, mg], w=[mg])
                    K.op("dve", lambda e: e.tensor_tensor(out=mgb[:], in0=mg[:], in1=pa_t[:], op=ALU.add), r=[mg, pa_t], w=[mgb])
                    for hh in range(2):
                        pH = K.ps()
                        for kc in range(8):
                            K.op("pe", lambda e: e.matmul(out=pH[:], lhsT=mgb[:, kc, :], rhs=wo[:, kc, hh * 512:(hh + 1) * 512],
                                                          start=(kc == 0), stop=(kc == 7)), r=[mgb, wo], w=[pH], inc=(kc == 7))
                        K.op("dve", lambda e: e.tensor_tensor(out=h1[:, hh * 512:(hh + 1) * 512], in0=pH[:], in1=x_t[:, hh * 512:(hh + 1) * 512],
                                                              op=ALU.add), r=[pH, x_t, h1], w=[h1])
                    K.dma("sp", s_h1[t0:t0 + 128, :], h1[:], r=[h1])
            K.barrier()

        if "2" in phases:
            with ExitStack() as sd:
                wqs = sbt(sd, "wqs", [128, 8, 2048], F32)
                skt = sbt(sd, "skt", [128, 16, 128], F32)
                pg = sbt(sd, "pg", [128, 8, 1024], BF16); pp = sbt(sd, "pp", [128, 2, 1024], BF16)
                gffn_b = bcast(sd, "gffn_b", gffn, 1024); gfin_b = bcast(sd, "gfin_b", gfin, 1024)
                iota_b = bcast(sd, "iota_b", iota_d, 256)
                gp = colvec(sd, "gp", gple, 8)
                for kc in range(8):
                    K.dma("sp", wqs[:, kc, :], wq[kc * 128:(kc + 1) * 128, :], w=[wqs])
                K.dma("sp", skt[:], skT.rearrange("s d n -> d s n"), w=[skt])
                with ExitStack() as sp_:
                    load_cast(sp_, pg, pgw, 8, 1024, scale_tile=gp)
                    load_cast(sp_, pp, ppw, 2, 1024)
                K.barrier()
                NB = 8
                H1 = [sbt(sd, f"h1D{b}", [128, 1024], F32) for b in range(2)]; junk = sbt(sd, "junkD", [128, 1024], BF16)
                ssP = sbt(sd, "ssP", [128, 1], F32); rstdP = sbt(sd, "rstdP", [128, 1], F32)
                ss = sbt(sd, "ssD", [128, 1], F32); rstd = sbt(sd, "rstdD", [128, 1], F32)
                XG = [sbt(sd, f"xgD{b}", [128, 1024], F32) for b in range(2)]; xgT = sbt(sd, "xgT", [128, 8, 128], F32)
                q_sb = sbt(sd, "q_sb", [128, 16, 128], F32); s_sb = sbt(sd, "s_sb", [128, 16, 128], F32)
                tmp128 = sbt(sd, "tmp128", [128, 128], F32)
                sv = sbt(sd, "svD", [128, 16, 16], F32); si = sbt(sd, "siD", [128, 16, 16], U32); sif = sbt(sd, "sifD", [128, 16, 16], F32)
                tmp256 = sbt(sd, "tmp256", [128, 256], F32)
                tops = sbt(sd, "tops", [128, 8, 16], F32); pos = sbt(sd, "posD", [128, 8, 16], U32)
                pi_u = sbt(sd, "pi_u", [128, 128], U32); pj_u = sbt(sd, "pj_u", [128, 128], U32)
                pi_f = sbt(sd, "pi_f", [128, 128], F32); pj_f = sbt(sd, "pj_f", [128, 128], F32)
                ei = sbt(sd, "eiD", [128, 128], F32); ej = sbt(sd, "ejD", [128, 128], F32)
                idxf = sbt(sd, "idxf", [128, 128], F32); IDX = [sbt(sd, f"idxD{b}", [128, 128], I32) for b in range(2)]
                GATE = [sbt(sd, f"gateD{b}", [128, 8, 16], F32) for b in range(2)]; gs8 = sbt(sd, "gs8", [128, 8], F32)
                hid = sbt(sd, "hidD", [128, 128], F32); h_x2 = sbt(sd, "h_x2", [128, 128], F32); h_sg = sbt(sd, "h_sg", [128, 128], F32)
                wts = sbt(sd, "wtsD", [128, 128], F32)
                Ug = [sbt(sd, f"Ug{b}", [128, 1024], F32) for b in range(NB)]
                Vg = Ug
                Vs = [sbt(sd, f"Vs{b}", [128, 1024], BF16) for b in range(2)]
                ttj = sbt(sd, "ttj", [128, 1024], F32)
                ttj2 = [ttj, sbt(sd, "ttjb", [128, 1024], F32)]
                h2 = sbt(sd, "h2D", [128, 1024], F32); xn3 = sbt(sd, "xn3", [128, 1024], BF16); xn3T = sbt(sd, "xn3T", [128, 8, 128], BF16)
                p_t = sbt(sd, "p_tD", [128, 256], F32); p_b = sbt(sd, "p_bD", [128, 256], BF16); pT = sbt(sd, "pTD", [128, 2, 128], BF16)
                o_t = ttj
                def prep(i):
                    h1 = H1[i % 2]; xg = XG[i % 2]; idx = IDX[i % 2]; gate = GATE[i % 2]; ss = ssP; rstd = rstdP
                    t0 = i * 128
                    K.dma("sp", h1[:], s_h1[t0:t0 + 128, :], w=[h1])
                    rms_rstd(h1, ss, rstd, junk)
                    K.op("dve", lambda e: e.scalar_tensor_tensor(out=xg[:], in0=h1[:], scalar=rstd[:, 0:1], in1=gffn_b[:],
                                                                 op0=ALU.mult, op1=ALU.mult), r=[h1, rstd, gffn_b], w=[xg])
                    yield
                    for hh in range(2):
                        pX = K.ps()
                        for c in range(4):
                            kc = hh * 4 + c
                            K.op("pe", lambda e: e.transpose(out=pX[:, c * 128:(c + 1) * 128], in_=xg[:, kc * 128:(kc + 1) * 128],
                                                             identity=ident[:]), r=[xg, ident], w=[pX], inc=(c == 3))
                        K.op("act", lambda e: e.activation(out=xgT[:, hh * 4:(hh + 1) * 4, :], in_=pX[:].rearrange("p (c t) -> p c t", c=4),
                                                           func=AF.Copy), r=[pX, xgT], w=[xgT])
                        yield
                    for g4 in range(4):
                        pQ = K.ps()
                        for j in range(4):
                            ch = g4 * 4 + j
                            for kc in range(8):
                                K.op("pe", lambda e: e.matmul(out=pQ[:, j * 128:(j + 1) * 128], lhsT=wqs[:, kc, ch * 128:(ch + 1) * 128],
                                                              rhs=xgT[:, kc, :], start=(kc == 0), stop=(kc == 7)),
                                     r=[wqs, xgT], w=[pQ], inc=(kc == 7 and j == 3))
                            yield
                        K.op("act", lambda e: e.activation(out=q_sb[:, g4 * 4:(g4 + 1) * 4, :], in_=pQ[:].rearrange("p (c t) -> p c t", c=4),
                                                           func=AF.Copy), r=[pQ, q_sb], w=[q_sb])
                        yield
                    for g4 in range(4):
                        pS2 = K.ps()
                        for j in range(4):
                            ch = g4 * 4 + j
                            K.op("pe", lambda e: e.matmul(out=pS2[:, j * 128:(j + 1) * 128], lhsT=q_sb[:, ch, :], rhs=skt[:, ch, :],
                                                          start=True, stop=True), r=[q_sb, skt], w=[pS2], inc=(j == 3))
                        K.op("act", lambda e: e.activation(out=s_sb[:, g4 * 4:(g4 + 1) * 4, :], in_=pS2[:].rearrange("p (c t) -> p c t", c=4),
                                                           func=AF.Copy), r=[pS2, s_sb], w=[s_sb])
                        yield
                    for st_ in range(16):
                        K.op("dve", lambda e: e.max(out=sv[:, st_, 0:8], in_=s_sb[:, st_, :]), r=[s_sb, sv], w=[sv])
                        K.op("dve", lambda e: e.match_replace(out=tmp128[:], in_to_replace=sv[:, st_, 0:8], in_values=s_sb[:, st_, :],
                                                              imm_value=-1e30), r=[sv, s_sb], w=[tmp128])
                        K.op("dve", lambda e: e.max(out=sv[:, st_, 8:16], in_=tmp128[:]), r=[tmp128, sv], w=[sv])
                        K.op("dve", lambda e: e.max_index(out=si[:, st_, 0:8], in_max=sv[:, st_, 0:8], in_values=s_sb[:, st_, :]),
                             r=[sv, s_sb, si], w=[si])
                        K.op("dve", lambda e: e.max_index(out=si[:, st_, 8:16], in_max=sv[:, st_, 8:16], in_values=s_sb[:, st_, :]),
                             r=[sv, s_sb, si], w=[si])
                        yield
                    K.op("dve", lambda e: e.tensor_copy(out=sif[:], in_=si[:]), r=[si], w=[sif])
                    CAND = s_sb[:].rearrange("p c t -> p (c t)").rearrange("p (h x) -> p h x", h=8)
                    sv5 = sv[:].rearrange("p (h c) k -> p h c k", c=2)
                    K.op("dve", lambda e: e.tensor_tensor(out=CAND.rearrange("p h (i j) -> p h i j", i=16),
                                                          in0=sv5[:, :, 0, :].unsqueeze(3).to_broadcast([128, 8, 16, 16]),
                                                          in1=sv5[:, :, 1, :].unsqueeze(2).to_broadcast([128, 8, 16, 16]), op=ALU.add),
                         r=[sv, s_sb], w=[s_sb])
                    for h in range(8):
                        K.op("dve", lambda e: e.max(out=tops[:, h, 0:8], in_=CAND[:, h, :]), r=[s_sb, tops], w=[tops])
                        K.op("dve", lambda e: e.match_replace(out=tmp256[:], in_to_replace=tops[:, h, 0:8], in_values=CAND[:, h, :],
                                                              imm_value=-1e30), r=[tops, s_sb], w=[tmp256])
                        K.op("dve", lambda e: e.max(out=tops[:, h, 8:16], in_=tmp256[:]), r=[tmp256, tops], w=[tops])
                        K.op("dve", lambda e: e.max_index(out=pos[:, h, 0:8], in_max=tops[:, h, 0:8], in_values=CAND[:, h, :]),
                             r=[tops, s_sb, pos], w=[pos])
                        K.op("dve", lambda e: e.max_index(out=pos[:, h, 8:16], in_max=tops[:, h, 8:16], in_values=CAND[:, h, :]),
                             r=[tops, s_sb, pos], w=[pos])
                        yield
                    posf = pos[:].rearrange("p h k -> p (h k)")
                    K.op("dve", lambda e: e.tensor_single_scalar(out=pi_u[:], in_=posf, scalar=4, op=ALU.logical_shift_right), r=[pos], w=[pi_u])
                    K.op("dve", lambda e: e.tensor_single_scalar(out=pj_u[:], in_=posf, scalar=15, op=ALU.bitwise_and), r=[pos], w=[pj_u])
                    K.op("dve", lambda e: e.tensor_copy(out=pi_f[:], in_=pi_u[:]), r=[pi_u], w=[pi_f])
                    K.op("dve", lambda e: e.tensor_copy(out=pj_f[:], in_=pj_u[:]), r=[pj_u], w=[pj_f])
                    sif5 = sif[:].rearrange("p (h c) k -> p h c k", c=2)
                    OH4 = q_sb[:].rearrange("p c t -> p (c t)").rearrange("p (h k i) -> p h k i", h=8, k=16)
                    io4 = iota_b[:].rearrange("p (k i) -> p k i", k=16).unsqueeze(1).to_broadcast([128, 8, 16, 16])
                    for (pf, cidx, eo) in ((pi_f, 0, ei), (pj_f, 1, ej)):
                        K.op("dve", lambda e: e.tensor_tensor(out=OH4, in0=io4,
                                                              in1=pf[:].rearrange("p (h k) -> p h k", h=8).unsqueeze(3).to_broadcast([128, 8, 16, 16]),
                                                              op=ALU.is_equal), r=[iota_b, pf, q_sb], w=[q_sb])
                        K.op("dve", lambda e: e.tensor_tensor(out=OH4, in0=OH4,
                                                              in1=sif5[:, :, cidx, :].unsqueeze(2).to_broadcast([128, 8, 16, 16]), op=ALU.mult),
                             r=[q_sb, sif], w=[q_sb])
                        K.op("dve", lambda e: e.tensor_reduce(out=eo[:], in_=OH4.rearrange("p h k i -> p (h k) i"), axis=AX.X, op=ALU.add),
                             r=[q_sb], w=[eo])
                        yield
                    K.op("dve", lambda e: e.scalar_tensor_tensor(out=idxf[:], in0=ei[:], scalar=128.0, in1=ej[:], op0=ALU.mult, op1=ALU.add),
                         r=[ei, ej], w=[idxf])
                    K.op("dve", lambda e: e.tensor_copy(out=idx[:], in_=idxf[:]), r=[idxf, idx], w=[idx])
                    yield
                    K.op("dve", lambda e: e.tensor_tensor(out=gate[:], in0=tops[:], in1=tops[:, :, 0:1].to_broadcast([128, 8, 16]), op=ALU.subtract),
                         r=[tops, gate], w=[gate])
                    K.op("act", lambda e: e.activation(out=gate[:], in_=gate[:], func=AF.Exp), r=[gate], w=[gate])
                    K.op("dve", lambda e: e.tensor_reduce(out=gs8[:], in_=gate[:], axis=AX.X, op=ALU.add), r=[gate], w=[gs8])
                    K.op("dve", lambda e: e.reciprocal(out=gs8[:], in_=gs8[:]), r=[gs8], w=[gs8])
                    K.op("dve", lambda e: e.tensor_tensor(out=gate[:], in0=gate[:], in1=gs8[:].unsqueeze(2).to_broadcast([128, 8, 16]), op=ALU.mult),
                         r=[gate, gs8], w=[gate])
                    yield

                K.nrot = 6
                for _ in prep(0):
                    pass
                for i in range(NT):
                    t0 = i * 128
                    h1 = H1[i % 2]; xg = XG[i % 2]; idx = IDX[i % 2]; gate = GATE[i % 2]; sgp = xg
                    g = prep(i + 1) if i + 1 < NT else iter(())
                    nstep = 0
                    K.dma("sp", p_t[:], p_in[t0:t0 + 128, :], w=[p_t])
                    for sl in range(128):
                        if nstep < 26:
                            nstep += 1
                            next(g, None)
                        ub = Ug[sl % NB]
                        K.dma("pool", None, None, r=[idx], w=[ub], fn=lambda e: e.indirect_dma_start(
                            out=ub[:], out_offset=None, in_=tab_u, in_offset=bass.IndirectOffsetOnAxis(ap=idx[:, sl:sl + 1], axis=0)))
                        tj = ttj2[sl % 2]
                        K.op("dve", lambda e: e.tensor_tensor(out=tj[:], in0=ub[:], in1=xg[:], op=ALU.mult), r=[ub, xg], w=[tj])
                        K.op("act", lambda e: e.activation(out=junk[:], in_=tj[:], func=AF.Copy, accum_out=hid[:, sl:sl + 1]),
                             r=[tj, hid], w=[junk, hid])
                    K.op("dve", lambda e: e.tensor_tensor(out=h_x2[:], in0=hid[:], in1=hid[:], op=ALU.mult), r=[hid], w=[h_x2])
                    K.op("dve", lambda e: e.tensor_scalar(out=h_x2[:], in0=h_x2[:], scalar1=0.044715, scalar2=1.0, op0=ALU.mult, op1=ALU.add),
                         r=[h_x2], w=[h_x2])
                    K.op("dve", lambda e: e.tensor_tensor(out=h_x2[:], in0=h_x2[:], in1=hid[:], op=ALU.mult), r=[h_x2, hid], w=[h_x2])
                    K.op("act", lambda e: e.activation(out=h_sg[:], in_=h_x2[:], func=AF.Sigmoid, scale=1.5957691216057308), r=[h_x2], w=[h_sg])
                    K.op("dve", lambda e: e.tensor_tensor(out=h_sg[:], in0=h_sg[:], in1=hid[:], op=ALU.mult), r=[h_sg, hid], w=[h_sg])
                    K.op("dve", lambda e: e.tensor_tensor(out=wts[:], in0=h_sg[:], in1=gate[:].rearrange("p h k -> p (h k)"), op=ALU.mult),
                         r=[h_sg, gate], w=[wts])
                    pF = [K.psb[6], K.psb[7]]
                    for sl in range(128):
                        next(g, None)
                        vb = Vg[(128 + sl) % NB]; vs = Vs[sl % 2]
                        K.dma("pool", None, None, r=[idx], w=[vb], fn=lambda e: e.indirect_dma_start(
                            out=vb[:], out_offset=None, in_=tab_v, in_offset=bass.IndirectOffsetOnAxis(ap=idx[:, sl:sl + 1], axis=0)))
                        K.op("act", lambda e: e.activation(out=vs[:], in_=vb[:], func=AF.Copy, scale=wts[:, sl:sl + 1]), r=[vb, wts], w=[vs])
                        for hh in range(2):
                            K.op("pe", lambda e: e.matmul(out=pF[hh][:], lhsT=identb[:], rhs=vs[:, hh * 512:(hh + 1) * 512],
                                                          start=(sl == 0), stop=(sl == 127)), r=[identb, vs], w=[pF[hh]], inc=(hh == 1))
                    for _ in g:
                        pass
                    for hh in range(2):
                        K.op("dve", lambda e: e.tensor_tensor(out=h2[:, hh * 512:(hh + 1) * 512], in0=pF[hh][:], in1=h1[:, hh * 512:(hh + 1) * 512],
                                                              op=ALU.add), r=[pF[hh], h1, h2], w=[h2])
                    rms_rstd(h2, ss, rstd, junk)
                    K.op("act", lambda e: e.activation(out=xn3[:], in_=h2[:], func=AF.Copy, scale=rstd[:, 0:1]), r=[h2, rstd], w=[xn3])
                    pX3 = K.ps(); pX3b = pX3[:].bitcast(BF16)
                    for kc in range(8):
                        K.op("pe", lambda e: e.transpose(out=pX3b[:, kc * 128:(kc + 1) * 128], in_=xn3[:, kc * 128:(kc + 1) * 128],
                                                         identity=identb[:]), r=[xn3, identb], w=[pX3], inc=(kc == 7))
                    K.op("act", lambda e: e.activation(out=xn3T[:], in_=pX3b[:, 0:1024].rearrange("p (c t) -> p c t", c=8), func=AF.Copy),
                         r=[pX3], w=[xn3T])
                    K.op("act", lambda e: e.activation(out=p_b[:], in_=p_t[:], func=AF.Copy), r=[p_t], w=[p_b])
                    pP = K.ps(); pPb = pP[:].bitcast(BF16)
                    for kc in range(2):
                        K.op("pe", lambda e: e.transpose(out=pPb[:, kc * 128:(kc + 1) * 128], in_=p_b[:, kc * 128:(kc + 1) * 128],
                                                         identity=identb[:]), r=[p_b, identb], w=[pP], inc=(kc == 1))
                    K.op("act", lambda e: e.activation(out=pT[:], in_=pPb[:, 0:256].rearrange("p (c t) -> p c t", c=2), func=AF.Copy),
                         r=[pP], w=[pT])
                    for hh in range(2):
                        pGt = K.ps()
                        for kc in range(8):
                            K.op("pe", lambda e: e.matmul(out=pGt[:], lhsT=xn3T[:, kc, :], rhs=pg[:, kc, hh * 512:(hh + 1) * 512],
                                                          start=(kc == 0), stop=(kc == 7)), r=[xn3T, pg], w=[pGt], inc=(kc == 7))
                        K.op("act", lambda e: e.activation(out=sgp[:, hh * 512:(hh + 1) * 512], in_=pGt[:], func=AF.Sigmoid), r=[pGt, sgp], w=[sgp])
                        pPP = K.ps()
                        for kc in range(2):
                            K.op("pe", lambda e: e.matmul(out=pPP[:], lhsT=pT[:, kc, :], rhs=pp[:, kc, hh * 512:(hh + 1) * 512],
                                                          start=(kc == 0), stop=(kc == 1)), r=[pT, pp], w=[pPP], inc=(kc == 1))
                        K.op("dve", lambda e: e.tensor_tensor(out=sgp[:, hh * 512:(hh + 1) * 512], in0=sgp[:, hh * 512:(hh + 1) * 512], in1=pPP[:],
                                                              op=ALU.mult), r=[sgp, pPP], w=[sgp])
                    K.op("dve", lambda e: e.tensor_tensor(out=h2[:], in0=h2[:], in1=sgp[:], op=ALU.add), r=[h2, sgp], w=[h2])
                    rms_rstd(h2, ss, rstd, junk)
                    K.op("dve", lambda e: e.scalar_tensor_tensor(out=o_t[:], in0=h2[:], scalar=rstd[:, 0:1], in1=gfin_b[:],
                                                                 op0=ALU.mult, op1=ALU.mult), r=[h2, rstd, gfin_b, o_t], w=[o_t])
                    K.dma("sp", out[t0:t0 + 128, :], o_t[:], r=[o_t])
            K.barrier()
        K.barrier()
    return nc


def host_inputs(inputs, b, T):
    f = lambda a: np.ascontiguousarray(np.asarray(a, dtype=np.float32))
    sk = np.asarray(inputs["peer_subkeys"], dtype=np.float32)[0]
    skT = np.ascontiguousarray(sk.transpose(0, 1, 3, 2).reshape(16, 128, 128))
    masks = np.zeros((128, 2), np.float32); masks[:64, 0] = 1.0; masks[64:, 1] = 1.0
    iota16 = np.tile(np.arange(16, dtype=np.float32), 16)
    return {
        "x": f(inputs["x"][b, :T]), "p": f(inputs["p"][0, b, :T]),
        "w_in": f(inputs["w_in"][0]), "conv_w": f(inputs["conv_w"][0]), "conv_b": f(inputs["conv_b"][0]),
        "shift_mu": f(inputs["shift_mu"][0]), "w0": f(inputs["w0"][0]), "w_up": f(inputs["w_up"][0]),
        "a0": f(inputs["a0"][0]), "a_up": f(inputs["a_up"][0]), "g_up": f(inputs["g_up"][0]),
        "k_k": f(inputs["k_k"][0]), "k_a": f(inputs["k_a"][0]), "r_k": f(inputs["r_k"][0].reshape(512)),
        "ln_g": f(inputs["ln_x_g"][0]), "ln_b": f(inputs["ln_x_b"][0]),
        "w_a": f(inputs["w_branch_a"][0]), "w_b": f(inputs["w_branch_b"][0]), "w_out": f(inputs["w_out"][0]),
        "gffn": f(inputs["norm_ffn_g"][0]), "wq": f(inputs["peer_wq"][0]), "skT": skT,
        "tab_u": f(inputs["peer_u"][0]), "tab_v": f(inputs["peer_v"][0]),
        "gple": f(inputs["norm_ple_g"][0]), "pgw": f(inputs["ple_gate_w"][0]), "ppw": f(inputs["ple_proj_w"][0]),
        "gfin": f(inputs["final_norm_g"]), "gmix": f(inputs["norm_mix_g"][0]),
        "ident": np.eye(128, dtype=np.float32), "masks": masks, "iota16": iota16,
    }


def kernel(**inputs):
    B, T = inputs["x"].shape[0], inputs["x"].shape[1]
    nc = build(T)
    in_maps = [host_inputs(inputs, b, T) for b in range(B)]
    res = run_bass_kernel_spmd(nc, in_maps, core_ids=list(range(B)))
    return np.stack([np.asarray(r["out"], dtype=np.float32) for r in res.results], axis=0)
```
